# Optimizing a Trainium2 kernel written in Bass

```python
import jax, jax.numpy as jnp
from jax import lax
import numpy as np

D_MODEL = 2048
BATCH = 2
SEQ = 8192
DEPTH = 2

N_META = 16
N_MIXERS = 2
CONV_WIDTH = 3
HEAD_DIM = 64
N_Q_HEADS = D_MODEL // HEAD_DIM
N_KV_HEADS = N_Q_HEADS // 8
GROUP = N_Q_HEADS // N_KV_HEADS
WINDOW = 128
BLOCK = 128
ROPE_THETA = 10000.0
D_FF = 4 * D_MODEL
RMS_EPS = 1e-5
NEG_INF = -1e30

kernel_name = "hybrid_shortconv_swa_sink_block"


def rms_norm(x, g):
    xf = x.astype(jnp.float32)
    var = jnp.mean(xf * xf, axis=-1, keepdims=True)
    return (xf * lax.rsqrt(var + RMS_EPS)).astype(x.dtype) * g


def short_conv_mixer(h, w_in, conv_w, w_out):
    bcu = h @ w_in
    b_gate, c_gate, u = jnp.split(bcu, 3, axis=-1)
    v = c_gate * u
    L = v.shape[1]
    vp = jnp.pad(v, ((0, 0), (CONV_WIDTH - 1, 0), (0, 0)))
    conv = conv_w[0] * vp[:, 0:L]
    for k in range(1, CONV_WIDTH):
        conv = conv + conv_w[k] * vp[:, k:k + L]
    return (b_gate * conv) @ w_out


def rope_tables(n_pos, offset):
    pos = jnp.arange(n_pos, dtype=jnp.float32) - offset
    inv = ROPE_THETA ** (-jnp.arange(0, HEAD_DIM, 2, dtype=jnp.float32) / HEAD_DIM)
    ang = pos[:, None] * inv[None, :]
    return jnp.cos(ang), jnp.sin(ang)


def apply_rope(x, cos, sin):
    x1, x2 = jnp.split(x, 2, axis=-1)
    c = cos[None, :, None, :]
    s = sin[None, :, None, :]
    return jnp.concatenate([x1 * c - x2 * s, x2 * c + x1 * s], axis=-1).astype(x.dtype)


def swa_sink_mixer(h, w_qkv, sinks, w_o):
    Bsz, L, _ = h.shape
    pad = (-L) % BLOCK
    P = L + pad
    nb = P // BLOCK
    hp = jnp.pad(h, ((0, 0), (pad, 0), (0, 0)))
    qkv = hp @ w_qkv
    q, k, v = jnp.split(qkv, [N_Q_HEADS * HEAD_DIM, (N_Q_HEADS + N_KV_HEADS) * HEAD_DIM], axis=-1)
    q = q.reshape(Bsz, P, N_Q_HEADS, HEAD_DIM)
    k = k.reshape(Bsz, P, N_KV_HEADS, HEAD_DIM)
    v = v.reshape(Bsz, P, N_KV_HEADS, HEAD_DIM)
    cos, sin = rope_tables(P, pad)
    q = apply_rope(q, cos, sin)
    k = apply_rope(k, cos, sin)

    qb = q.reshape(Bsz, nb, BLOCK, N_KV_HEADS, GROUP, HEAD_DIM)
    kb = k.reshape(Bsz, nb, BLOCK, N_KV_HEADS, HEAD_DIM)
    vb = v.reshape(Bsz, nb, BLOCK, N_KV_HEADS, HEAD_DIM)
    zpad = ((0, 0), (1, 0), (0, 0), (0, 0), (0, 0))
    k_band = jnp.concatenate([jnp.pad(kb[:, :-1], zpad), kb], axis=2)
    v_band = jnp.concatenate([jnp.pad(vb[:, :-1], zpad), vb], axis=2)

    scale = HEAD_DIM ** -0.5
    s = jnp.einsum("bnqhgd,bnkhd->bhgnqk", qb, k_band).astype(jnp.float32) * scale

    blk = jnp.arange(nb)[:, None, None] * BLOCK
    q_idx = blk + jnp.arange(BLOCK)[None, :, None]
    k_idx = blk - BLOCK + jnp.arange(2 * BLOCK)[None, None, :]
    diff = q_idx - k_idx
    allowed = (diff >= 0) & (diff < WINDOW) & (k_idx >= pad)
    s = jnp.where(allowed[None, None, None], s, NEG_INF)

    sink = sinks.astype(jnp.float32).reshape(N_KV_HEADS, GROUP)[None, :, :, None, None, None]
    m = jnp.maximum(jnp.max(s, axis=-1, keepdims=True), sink)
    e = jnp.exp(s - m)
    den = jnp.sum(e, axis=-1, keepdims=True) + jnp.exp(sink - m)
    p = (e / den).astype(v.dtype)

    o = jnp.einsum("bhgnqk,bnkhd->bnqhgd", p, v_band).reshape(Bsz, P, N_Q_HEADS * HEAD_DIM)
    return o[:, pad:] @ w_o


def squared_relu_mlp(h, w_up, w_down):
    a = jax.nn.relu(h @ w_up)
    return (a * a) @ w_down


def setup_inputs(seed: int = 0) -> dict:
    key = jax.random.key(seed)
    ks = jax.random.split(key, 20)
    D = D_MODEL
    f32 = jnp.float32

    def nrm(k, shape, scale):
        return jax.random.normal(k, shape, f32) * scale

    def gain(k):
        return jnp.ones((D,), f32) + 0.02 * jax.random.normal(k, (D,), f32)

    return {
        "x": nrm(ks[0], (BATCH, SEQ, D), 1.0),
        "meta_tokens": nrm(ks[1], (N_META, D), 1.0),
        "norm_mix_0": gain(ks[2]),
        "w_in_conv": nrm(ks[3], (D, 3 * D), D ** -0.5),
        "conv_w": nrm(ks[4], (CONV_WIDTH, D), CONV_WIDTH ** -0.5),
        "w_out_conv": nrm(ks[5], (D, D), D ** -0.5),
        "norm_mlp_0": gain(ks[6]),
        "w_up_0": nrm(ks[7], (D, D_FF), D ** -0.5),
        "w_down_0": nrm(ks[8], (D_FF, D), D_FF ** -0.5),
        "norm_mix_1": gain(ks[9]),
        "w_qkv": nrm(ks[10], (D, (N_Q_HEADS + 2 * N_KV_HEADS) * HEAD_DIM), D ** -0.5),
        "attn_sinks": nrm(ks[11], (N_Q_HEADS,), 0.5),
        "w_o": nrm(ks[12], (N_Q_HEADS * HEAD_DIM, D), (N_Q_HEADS * HEAD_DIM) ** -0.5),
        "norm_mlp_1": gain(ks[13]),
        "w_up_1": nrm(ks[14], (D, D_FF), D ** -0.5),
        "w_down_1": nrm(ks[15], (D_FF, D), D_FF ** -0.5),
        "norm_final": gain(ks[16]),
    }


def reference(x, meta_tokens, norm_mix_0, w_in_conv, conv_w, w_out_conv, norm_mlp_0, w_up_0, w_down_0,
              norm_mix_1, w_qkv, attn_sinks, w_o, norm_mlp_1, w_up_1, w_down_1, norm_final):
    Bsz = x.shape[0]
    meta = jnp.broadcast_to(meta_tokens[None].astype(x.dtype), (Bsz, N_META, D_MODEL))
    h = jnp.concatenate([meta, x], axis=1)

    mixers = [
        lambda t: short_conv_mixer(t, w_in_conv, conv_w, w_out_conv),
        lambda t: swa_sink_mixer(t, w_qkv, attn_sinks, w_o),
    ]
    mix_norms = [norm_mix_0, norm_mix_1]
    mlps = [(norm_mlp_0, w_up_0, w_down_0), (norm_mlp_1, w_up_1, w_down_1)]

    for i in range(DEPTH):
        h = h + mixers[i % N_MIXERS](rms_norm(h, mix_norms[i]))
        g, wu, wd = mlps[i]
        h = h + squared_relu_mlp(rms_norm(h, g), wu, wd)

    out = rms_norm(h, norm_final)
    return out[:, N_META:]
```

```python
import os
import numpy as np
import ml_dtypes
import concourse.bass as bass
import concourse.mybir as mybir
from concourse.bass_utils import run_bass_kernel_spmd

F32 = mybir.dt.float32
BF16 = mybir.dt.bfloat16
ALU = mybir.AluOpType
AF = mybir.ActivationFunctionType
AX = mybir.AxisListType

D = 2048
NCH = 16
DFF = 8192
SEQ = 8192
NMETA = 16
HALO = 130
TOK_CORE = 2048
NT_TILES = 4
TQ = 512
TMAX = HALO + TQ
XCOLS = TOK_CORE + HALO
KCOLS = 640
RING = 5
SLOT = 4096
EPS = 1e-5
NEG = -1e30
NWT = 32 + 8 + 64 + 21 + 64


class DSem:
    def __init__(self, sem):
        self.sem = sem
        self.count = 0


class Op:
    __slots__ = ("eng", "fn", "raw", "other", "signal", "is_dma", "dsem", "dval", "epoch", "sval", "ndma")

    def __init__(self, eng, fn):
        self.eng = eng
        self.fn = fn
        self.raw = []
        self.other = []
        self.signal = False
        self.is_dma = False
        self.dsem = None
        self.dval = 0
        self.epoch = 0
        self.sval = 0
        self.ndma = 0


class Prog:
    ENGS = ("pe", "act", "dve", "pool", "sp")

    def __init__(self):
        self.q = {e: [] for e in self.ENGS}
        self.wr = {}
        self.rd = {}
        self.epoch = 0

    def add(self, eng, fn, reads=(), writes=(), dsem=None, ndma=0):
        op = Op(eng, fn)
        op.epoch = self.epoch
        if dsem is not None:
            op.is_dma = True
            op.dsem = dsem
            op.ndma = ndma
            dsem.count += 16 * ndma
            op.dval = dsem.count
        raw, other = [], []
        psr = [r for r in reads if isinstance(r[0], tuple) and r[0][0] == "ps"]
        if psr:
            writes = list(writes) + [r for r in psr if r not in writes]
        for (k, a, b) in reads:
            for (wa, wb, w) in self.wr.get(k, ()):
                if wa < b and a < wb:
                    raw.append(w)
        for (k, a, b) in writes:
            for (wa, wb, w) in self.wr.get(k, ()):
                if wa < b and a < wb:
                    other.append(w)
            for (ra, rb, r) in self.rd.get(k, ()):
                if ra < b and a < rb:
                    other.append(r)
        op.raw = [d for d in dict.fromkeys(raw) if d is not op]
        op.other = [d for d in dict.fromkeys(other) if d is not op]
        for (k, a, b) in reads:
            self.rd.setdefault(k, []).append((a, b, op))
        for (k, a, b) in writes:
            self.wr[k] = [x for x in self.wr.get(k, ()) if not (a <= x[0] and x[1] <= b)] + [(a, b, op)]
            self.rd[k] = [x for x in self.rd.get(k, ()) if not (a <= x[0] and x[1] <= b)]
        self.q[eng].append(op)
        return op

    def finalize_and_emit(self, nc, sems):
        for e in self.ENGS:
            for op in self.q[e]:
                deps = []
                for d in op.raw:
                    if d.is_dma or d.eng != op.eng or op.eng != "pe":
                        deps.append(d)
                for d in op.other:
                    if d.is_dma or d.eng != op.eng or op.eng != "pe":
                        deps.append(d)
                op.raw = deps
                op.other = []
                for d in deps:
                    if not d.is_dma:
                        d.signal = True
        for e in self.ENGS:
            cnt = {}
            for op in self.q[e]:
                if op.signal and not op.is_dma:
                    cnt[op.epoch] = cnt.get(op.epoch, 0) + 1
                    op.sval = cnt[op.epoch]
        handles = {"pe": nc.tensor, "act": nc.scalar, "dve": nc.vector, "pool": nc.gpsimd, "sp": nc.sync}

        def emit(ename, e):
            waited = {}
            for op in self.q[ename]:
                for d in op.raw:
                    if d.is_dma:
                        s, v = d.dsem.sem, d.dval
                    else:
                        s, v = sems[(d.eng, d.epoch)], d.sval
                    key = id(s)
                    if waited.get(key, 0) < v:
                        e.wait_ge(s, v)
                        waited[key] = v
                ins = op.fn(e)
                if op.signal and not op.is_dma:
                    ins.then_inc(sems[(op.eng, op.epoch)], 1)

        with nc.Block() as block:
            @block.tensor
            def _(e):
                emit("pe", e)

            @block.scalar
            def _(e):
                emit("act", e)

            @block.vector
            def _(e):
                emit("dve", e)

            @block.gpsimd
            def _(e):
                emit("pool", e)

            @block.sync
            def _(e):
                emit("sp", e)


def subtiles(a, b, n=512):
    out = []
    while a < b:
        out.append((a, min(a + n, b)))
        a += n
    return out


def build_program(dbg=""):
    nc = bass.Bass("TRN2", target_bir_lowering=False)
    xT = nc.dram_tensor("xT", [NCH, 128, XCOLS], F32, kind="ExternalInput").ap()
    wts = nc.dram_tensor("wts", [NWT, 128, SLOT], F32, kind="ExternalInput").ap()
    cosd = nc.dram_tensor("cosd", [128, 2176], F32, kind="ExternalInput").ap()
    sind = nc.dram_tensor("sind", [128, 2176], F32, kind="ExternalInput").ap()
    maskd = nc.dram_tensor("maskd", [128, 2, 256], F32, kind="ExternalInput").ap()
    cstd = nc.dram_tensor("cstd", [128, 160], F32, kind="ExternalInput").ap()
    identd = nc.dram_tensor("identd", [128, 128], BF16, kind="ExternalInput").ap()
    outT = nc.dram_tensor("outT", [NCH, 128, TOK_CORE], F32, kind="ExternalOutput").ap()

    from contextlib import ExitStack
    es = ExitStack()
    with es:
        def sb(name, shape, dt):
            return es.enter_context(nc.sbuf_tensor(name, shape, dt))

        def sem(name):
            return es.enter_context(nc.semaphore(name))

        h = sb("h", [128, NCH, TMAX], F32)
        xn = sb("xn", [128, NCH, TMAX], BF16)
        G = sb("G", [128, NCH, TMAX], BF16)
        kbuf = sb("kbuf", [128, 8, KCOLS], BF16)
        vbuf = sb("vbuf", [128, 5, 256], BF16)
        qbuf = sb("qbuf", [128, 16, TQ], BF16)
        cosT = sb("cosT", [128, KCOLS], F32)
        sinT = sb("sinT", [128, KCOLS], F32)
        masks = sb("masks", [128, 2, 256], F32)
        cst = sb("cst", [128, 160], F32)
        negsink = sb("negsink", [128, 32], F32)
        ident = sb("ident", [128, 128], BF16)
        ones = sb("ones", [128, 128], BF16)
        epsc = sb("epsc", [128, 1], F32)
        wring = sb("wring", [128, RING, SLOT], BF16)
        NSQ = 3
        sq = sb("sq", [128, NSQ, TMAX], BF16)
        rstd = sb("rstd", [128, TMAX], F32)
        vtail = sb("vtail", [128, NCH, 2], F32)
        b_sb = sb("b_sb", [128, 2, TMAX], F32)
        c_sb = sb("c_sb", [128, 2, TMAX], F32)
        v_sb = sb("v_sb", [128, 2, TMAX + 2], F32)
        t_sb = sb("t_sb", [128, 2, TMAX], F32)
        scr = sb("scr", [128, 4, 512], F32)
        AW = 4
        sm = sb("sm", [128, AW, 2, 257], F32)
        pp = sb("pp", [128, AW, 2, 256], BF16)
        pT = sb("pT", [128, AW, 4, 128], BF16)
        stat = sb("stat", [128, AW, 8], F32)

        ps = es.enter_context(nc.psum_tensor("ps", [128, 8, 512], F32))

        P = Prog()
        sems = {}
        for e in Prog.ENGS:
            for ep in range(NT_TILES + 1):
                sems[(e, ep)] = sem(f"s_{e}_{ep}")
        ds_w = [DSem(sem(f"dw{i}")) for i in range(RING)]
        ds_x = DSem(sem("dx"))
        ds_tab = DSem(sem("dtab"))
        ds_c = DSem(sem("dc"))
        ds_out = [DSem(sem(f"do{i}")) for i in range(2)]

        gain = lambda n, c: cst[:, n * 16 + c: n * 16 + c + 1]
        convw = lambda k, c: cst[:, 80 + k * 16 + c: 80 + k * 16 + c + 1]
        sinks = cst[:, 128:160]

        bank_ctr = [0]

        def bank():
            b = bank_ctr[0] % 6
            bank_ctr[0] += 1
            return b

        def R(key, a=0, b=1):
            if isinstance(key, tuple) and key[0] == "ps":
                return (key, 0, 512)
            return (key, a, b)

        wctr = [0]
        PLAN = []

        def wnext(tile_t, key, ncols_used=SLOT):
            n = wctr[0]
            wctr[0] += 1
            slot = n % RING
            widx = n % NWT
            if tile_t == 0:
                assert len(PLAN) == widx
                PLAN.append(key)
            else:
                assert PLAN[widx] == key, (PLAN[widx], key)
            res = R(("w", slot))

            def fn(e, slot=slot, widx=widx, nc_=ncols_used):
                return e.dma_start(out=wring[:, slot, 0:nc_], in_=wts[widx, :, 0:nc_]).then_inc(ds_w[slot].sem, 16)
            P.add("pool", fn, writes=[res], dsem=ds_w[slot], ndma=1)
            return slot, res

        def mm_group(out_ap, pairs, reads, writes, **kw):
            def fn(e):
                n = len(pairs)
                ins = None
                for i, (l, r) in enumerate(pairs):
                    ins = e.matmul(out_ap, lhsT=l, rhs=r, start=(i == 0), stop=(i == n - 1), **kw)
                return ins
            return P.add("pe", fn, reads=reads, writes=writes)

        def simple(eng, fn, reads, writes):
            return P.add(eng, fn, reads=reads, writes=writes)

        def mm_staggered(descs):
            banks = [bank() for _ in descs]
            K = len(descs[0][1])
            for k in range(K):
                def fn(e, k=k):
                    ins = None
                    for (n, pairs, pr, cr), bk in zip(descs, banks):
                        ins = e.matmul(ps[:, bk, 0:n], lhsT=pairs[k][0], rhs=pairs[k][1], start=(k == 0), stop=(k == K - 1))
                    return ins
                reads = []
                for (n, pairs, pr, cr) in descs:
                    reads += list(pr[k]) + list(cr)
                P.add("pe", fn, reads=list(dict.fromkeys(reads)), writes=[R(("ps", bk)) for bk in banks])
            return banks

        def ld_consts(e):
            e.dma_start(out=cst[:], in_=cstd[:, :]).then_inc(ds_c.sem, 16)
            e.dma_start(out=masks[:], in_=maskd[:, :, :]).then_inc(ds_c.sem, 16)
            return e.dma_start(out=ident[:], in_=identd[:, :]).then_inc(ds_c.sem, 16)
        P.add("sp", ld_consts, writes=[R("cst"), R("masks"), R("ident")], dsem=ds_c, ndma=3)
        simple("dve", lambda e: e.memset(ones[:], 1.0), [], [R("ones")])
        simple("dve", lambda e: e.memset(epsc[:], float(EPS)), [], [R("epsc")])
        simple("dve", lambda e: e.memset(vtail[:], 0.0), [], [R(("vtail", c)) for c in range(NCH)])
        simple("dve", lambda e: e.tensor_scalar(out=negsink[:], in0=sinks, scalar1=8.0, scalar2=None, op0=ALU.mult),
               [R("cst")], [R("negsink")])

        class StatAcc:
            def __init__(self, a, b, banks=None, src=None, rdst=None):
                self.a, self.b = a, b
                self.subs = subtiles(a, b)
                self.banks = banks if banks is not None else [6, 7][:len(self.subs)]
                self.src = src
                self.rdst = rdst
                self.n = 0
                self.pending = []

            def feed(self, c, eng="act"):
                a, b = self.a, self.b
                idx = self.n
                slot = idx % NSQ
                self.n += 1
                if self.src is not None:
                    sap, sres = self.src(c)
                else:
                    sap, sres = h[:, c, a:b], [R(("h", c), a, b)]
                if eng == "act":
                    simple("act", lambda e: e.activation(out=sq[:, slot, a:b], in_=sap, func=AF.Square),
                           sres, [R(("sq", slot), a, b)])
                else:
                    simple("dve", lambda e: e.tensor_tensor(out=sq[:, slot, a:b], in0=sap, in1=sap, op=ALU.mult),
                           sres, [R(("sq", slot), a, b)])
                subs, banks = self.subs, self.banks

                def pe_thunk():
                    def fn(e):
                        ins = None
                        for (sa, sb_), bk in zip(subs, banks):
                            ins = e.matmul(ps[:, bk, 0:sb_ - sa], lhsT=ones[:], rhs=sq[:, slot, sa:sb_],
                                           start=(idx == 0), stop=(idx == NCH - 1))
                        return ins
                    P.add("pe", fn, reads=[R(("sq", slot), a, b), R("ones")], writes=[R(("ps", bk)) for bk in banks])
                self.pending.append(pe_thunk)

            def flush(self, keep=0):
                while len(self.pending) > keep:
                    self.pending.pop(0)()

            def finish(self):
                assert self.n == NCH
                self.flush(0)
                rap, rkey = self.rdst if self.rdst is not None else (rstd, "rstd")
                for (sa, sb_), bk in zip(self.subs, self.banks):
                    simple("act", lambda e, sa=sa, sb_=sb_, bk=bk: e.activation(
                        out=rap[:, sa:sb_], in_=ps[:, bk, 0:sb_ - sa], func=AF.Ln, bias=epsc[:, 0:1],
                        scale=float(1.0 / D)),
                        [R(("ps", bk)), R("epsc")], [R(rkey, sa, sb_)])
                    simple("act", lambda e, sa=sa, sb_=sb_: e.activation(
                        out=rap[:, sa:sb_], in_=rap[:, sa:sb_], func=AF.Exp, scale=-0.5),
                        [R(rkey, sa, sb_)], [R(rkey, sa, sb_)])

        def norm_apply_xn(nidx, a, b, src=None, rs=None):
            rap, rkey = rs if rs is not None else (rstd, "rstd")
            for c in range(NCH):
                if src is not None:
                    sap, sres = src(c)
                else:
                    sap, sres = h[:, c, a:b], [R(("h", c), a, b)]
                simple("dve",
                       lambda e, c=c, sap=sap: e.scalar_tensor_tensor(out=xn[:, c, a:b], in0=sap, scalar=gain(nidx, c),
                                                                      in1=rap[:, a:b], op0=ALU.mult, op1=ALU.mult),
                       list(sres) + [R(rkey, a, b), R("cst")], [R(("xn", c), a, b)])

        def mlp(tile_t, a, b, lyr, acc, hooks=None):
            subs = subtiles(a, b)
            rctr = [0]

            def up(blk):
                bi = blk % 2
                wu = [wnext(tile_t, ("up", lyr, blk, 0)), wnext(tile_t, ("up", lyr, blk, 1))]
                groups = []
                for j in range(4):
                    slot, wres = wu[j // 2]
                    coff = (j % 2) * 128
                    for (sa, sb_) in subs:
                        pairs = [(wring[:, slot, kc * 256 + coff: kc * 256 + coff + 128], xn[:, kc, sa:sb_])
                                 for kc in range(NCH)]
                        groups.append((j, sa, sb_, pairs, wres))
                nst = min(len(groups), 6) if blk == 0 else 0
                banks = {}
                if nst:
                    bl = mm_staggered([(sb_ - sa, pairs, [[R(("xn", kc), sa, sb_)] for kc in range(NCH)], [wres])
                                       for (j, sa, sb_, pairs, wres) in groups[:nst]])
                    for gi in range(nst):
                        banks[gi] = bl[gi]
                for gi, (j, sa, sb_, pairs, wres) in enumerate(groups):
                    n = sb_ - sa
                    if gi in banks:
                        bk = banks[gi]
                    else:
                        bk = bank()
                        mm_group(ps[:, bk, 0:n], pairs,
                                 [wres] + [R(("xn", kc), sa, sb_) for kc in range(NCH)], [R(("ps", bk))])
                    ri = rctr[0] % 2
                    rctr[0] += 1
                    simple("act",
                           lambda e, bk=bk, n=n, ri=ri: e.activation(out=scr[:, ri, 0:n], in_=ps[:, bk, 0:n], func=AF.Relu),
                           [R(("ps", bk))], [R(("scr", ri))])
                    simple("act",
                           lambda e, n=n, ri=ri, bi=bi, j=j, sa=sa, sb_=sb_: e.activation(
                               out=G[:, bi * 4 + j, sa:sb_], in_=scr[:, ri, 0:n], func=AF.Square),
                           [R(("scr", ri))], [R(("G", bi * 4 + j), sa, sb_)])

            def down(blk):
                bi = blk % 2
                wd = [wnext(tile_t, ("down", lyr, blk, 0)), wnext(tile_t, ("down", lyr, blk, 1))]
                for m2 in range(NCH):
                    for (sa, sb_) in subs:
                        bk = bank()
                        n = sb_ - sa
                        pairs = [(wring[:, wd[j // 2][0], (j % 2) * 2048 + m2 * 128: (j % 2) * 2048 + m2 * 128 + 128],
                                  G[:, bi * 4 + j, sa:sb_]) for j in range(4)]
                        mm_group(ps[:, bk, 0:n], pairs,
                                 [wd[0][1], wd[1][1]] + [R(("G", bi * 4 + j), sa, sb_) for j in range(4)],
                                 [R(("ps", bk))])
                        simple("dve",
                               lambda e, bk=bk, n=n, m2=m2, sa=sa, sb_=sb_: e.tensor_tensor(
                                   out=h[:, m2, sa:sb_], in0=ps[:, bk, 0:n], in1=h[:, m2, sa:sb_], op=ALU.add),
                               [R(("ps", bk)), R(("h", m2), sa, sb_)], [R(("h", m2), sa, sb_)])
                        if blk == 15:
                            acc.flush(keep=2)
                    if blk == 15:
                        acc.feed(m2)

            up(0)
            for blk in range(16):
                if hooks and blk in hooks:
                    hooks[blk]()
                if blk + 1 < 16:
                    up(blk + 1)
                down(blk)

        def l0_mixer(tile_t, T, acc, side=None):
            subs = subtiles(0, T)
            for m in range(NCH):
                mi = m % 2
                wa = wnext(tile_t, ("win", m, 0), 3072)
                wb = wnext(tile_t, ("win", m, 1), 3072)

                def wl(kc, off):
                    slot = wa[0] if kc < 8 else wb[0]
                    kk = kc % 8
                    return wring[:, slot, kk * 384 + off: kk * 384 + off + 128]
                groups = []
                for kind, off in (("c", 128), ("u", 256), ("b", 0)):
                    for (sa, sb_) in subs:
                        pairs = [(wl(kc, off), xn[:, kc, sa:sb_]) for kc in range(NCH)]
                        groups.append((kind, sa, sb_, pairs))
                stag = None
                if m == 0:
                    stag = mm_staggered([(sb_ - sa, pairs, [[R(("xn", kc), sa, sb_)] for kc in range(NCH)], [wa[1], wb[1]])
                                         for (kind, sa, sb_, pairs) in groups])
                for gi, (kind, sa, sb_, pairs) in enumerate(groups):
                    if True:
                        n = sb_ - sa
                        if stag is not None:
                            bk = stag[gi]
                        else:
                            bk = bank()
                            mm_group(ps[:, bk, 0:n], pairs,
                                     [wa[1], wb[1]] + [R(("xn", kc), sa, sb_) for kc in range(NCH)], [R(("ps", bk))])
                        if kind == "c":
                            simple("act", lambda e, bk=bk, n=n, sa=sa, sb_=sb_, mi=mi: e.activation(
                                out=c_sb[:, mi, sa:sb_], in_=ps[:, bk, 0:n], func=AF.Copy),
                                [R(("ps", bk))], [R(("c_sb", mi), sa, sb_)])
                        elif kind == "b":
                            simple("act", lambda e, bk=bk, n=n, sa=sa, sb_=sb_, mi=mi: e.activation(
                                out=b_sb[:, mi, sa:sb_], in_=ps[:, bk, 0:n], func=AF.Copy),
                                [R(("ps", bk))], [R(("b_sb", mi), sa, sb_)])
                        else:
                            simple("dve", lambda e, bk=bk, n=n, sa=sa, sb_=sb_, mi=mi: e.tensor_tensor(
                                out=v_sb[:, mi, 2 + sa:2 + sb_], in0=ps[:, bk, 0:n], in1=c_sb[:, mi, sa:sb_], op=ALU.mult),
                                [R(("ps", bk)), R(("c_sb", mi), sa, sb_)], [R(("v_sb", mi), 2 + sa, 2 + sb_)])
                simple("dve", lambda e, mi=mi, m=m: e.tensor_copy(out=v_sb[:, mi, 0:2], in_=vtail[:, m, :]),
                       [R(("vtail", m))], [R(("v_sb", mi), 0, 2)])
                simple("dve", lambda e, mi=mi, m=m: e.tensor_scalar(
                    out=t_sb[:, mi, 0:T], in0=v_sb[:, mi, 2:2 + T], scalar1=convw(2, m), scalar2=None, op0=ALU.mult),
                    [R(("v_sb", mi), 2, 2 + T), R("cst")], [R(("t_sb", mi), 0, T)])
                simple("dve", lambda e, mi=mi, m=m: e.scalar_tensor_tensor(
                    out=t_sb[:, mi, 0:T], in0=v_sb[:, mi, 1:1 + T], scalar=convw(1, m), in1=t_sb[:, mi, 0:T],
                    op0=ALU.mult, op1=ALU.add),
                    [R(("v_sb", mi), 1, 1 + T), R(("t_sb", mi), 0, T), R("cst")], [R(("t_sb", mi), 0, T)])
                simple("dve", lambda e, mi=mi, m=m: e.scalar_tensor_tensor(
                    out=t_sb[:, mi, 0:T], in0=v_sb[:, mi, 0:T], scalar=convw(0, m), in1=t_sb[:, mi, 0:T],
                    op0=ALU.mult, op1=ALU.add),
                    [R(("v_sb", mi), 0, T), R(("t_sb", mi), 0, T), R("cst")], [R(("t_sb", mi), 0, T)])
                simple("dve", lambda e, mi=mi, m=m: e.tensor_copy(out=vtail[:, m, :], in_=v_sb[:, mi, T:T + 2]),
                       [R(("v_sb", mi), T, T + 2)], [R(("vtail", m))])
                simple("dve", lambda e, mi=mi, m=m: e.tensor_tensor(
                    out=G[:, m, 0:T], in0=b_sb[:, mi, 0:T], in1=t_sb[:, mi, 0:T], op=ALU.mult),
                    [R(("b_sb", mi), 0, T), R(("t_sb", mi), 0, T)], [R(("G", m), 0, T)])
                if side:
                    side.pop(0)()
            while side:
                side.pop(0)()
            for i in range(8):
                slot, wres = wnext(tile_t, ("wout", i))
                for cc in range(2):
                    m2 = 2 * i + cc
                    for (sa, sb_) in subs:
                        bk = bank()
                        n = sb_ - sa
                        pairs = [(wring[:, slot, kc * 256 + cc * 128: kc * 256 + cc * 128 + 128], G[:, kc, sa:sb_])
                                 for kc in range(NCH)]
                        mm_group(ps[:, bk, 0:n], pairs, [wres] + [R(("G", kc), sa, sb_) for kc in range(NCH)],
                                 [R(("ps", bk))])
                        simple("dve", lambda e, bk=bk, n=n, m2=m2, sa=sa, sb_=sb_: e.tensor_tensor(
                            out=h[:, m2, sa:sb_], in0=ps[:, bk, 0:n], in1=h[:, m2, sa:sb_], op=ALU.add),
                            [R(("ps", bk)), R(("h", m2), sa, sb_)], [R(("h", m2), sa, sb_)])
                        acc.flush(keep=1)
                    acc.feed(m2)

        def rope_pair(bkA, bkB, n, tc0, outA, outB, resA, resB):
            cs = cosT[:, tc0:tc0 + n]
            sn = sinT[:, tc0:tc0 + n]
            tabs = [R("tab", tc0, tc0 + n)]
            A = ps[:, bkA, 0:n]
            B = ps[:, bkB, 0:n]
            simple("dve", lambda e: e.tensor_tensor(out=scr[:, 0, 0:n], in0=A, in1=cs, op=ALU.mult),
                   [R(("ps", bkA))] + tabs, [R(("scr", 0))])
            simple("dve", lambda e: e.tensor_tensor(out=scr[:, 1, 0:n], in0=B, in1=sn, op=ALU.mult),
                   [R(("ps", bkB))] + tabs, [R(("scr", 1))])
            simple("dve", lambda e: e.tensor_tensor(out=scr[:, 2, 0:n], in0=B, in1=cs, op=ALU.mult),
                   [R(("ps", bkB))] + tabs, [R(("scr", 2))])
            simple("dve", lambda e: e.tensor_tensor(out=scr[:, 3, 0:n], in0=A, in1=sn, op=ALU.mult),
                   [R(("ps", bkA))] + tabs, [R(("scr", 3))])
            simple("dve", lambda e: e.tensor_tensor(out=outA, in0=scr[:, 0, 0:n], in1=scr[:, 1, 0:n], op=ALU.subtract),
                   [R(("scr", 0)), R(("scr", 1))], [resA])
            simple("dve", lambda e: e.tensor_tensor(out=outB, in0=scr[:, 2, 0:n], in1=scr[:, 3, 0:n], op=ALU.add),
                   [R(("scr", 2)), R(("scr", 3))], [resB])

        def att_step(sidx, i, j, pr, mk):
            w = sidx % AW
            g = j // 2
            bS = [2 * w, 2 * w + 1]
            osl = sidx % 4
            hcol = 4 * j + 2 * pr
            ch = 2 * j + pr
            psTv = ps[:, 2 * w, 256:512].bitcast(BF16)
            rS = [R(("ps", bS[0]), 0, 256), R(("ps", bS[1]), 0, 256)]
            rT = R(("ps", bS[0]))
            rO = R(("ps", bS[1]))

            def st_scores():
                for hh in range(2):
                    s_ = 2 * pr + hh
                    prt = slice(32 * s_, 32 * s_ + 32)
                    pairs = [(qbuf[prt, 2 * j, 128 * i:128 * i + 128], kbuf[prt, 2 * g, 128 * i:128 * i + 256]),
                             (qbuf[prt, 2 * j + 1, 128 * i:128 * i + 128], kbuf[prt, 2 * g + 1, 128 * i:128 * i + 256])]
                    mm_group(ps[:, bS[hh], 0:256], pairs,
                             [R(("q", 2 * j)), R(("q", 2 * j + 1)),
                              R(("k", 2 * g), 128 * i, 128 * i + 256), R(("k", 2 * g + 1), 128 * i, 128 * i + 256)],
                             [rS[hh]], tile_position=(32 * s_, 0))

            def st_mask():
                for hh in range(2):
                    simple("dve", lambda e, hh=hh: e.tensor_tensor(
                        out=sm[:, w, hh, 0:256], in0=ps[:, bS[hh], 0:256], in1=masks[:, mk, :], op=ALU.add),
                        [rS[hh], R("masks")], [R(("sm", w, hh))])
                simple("dve", lambda e: e.tensor_copy(out=sm[:, w, :, 256], in_=negsink[:, hcol:hcol + 2]),
                       [R("negsink")], [R(("smk", w))])

            def st_max():
                simple("dve", lambda e: e.tensor_reduce(out=stat[:, w, 0:2], in_=sm[:, w, :, :], axis=AX.X, op=ALU.max),
                       [R(("sm", w, 0)), R(("sm", w, 1)), R(("smk", w))], [R(("st", w, 0))])

            def st_negm():
                simple("dve", lambda e: e.tensor_scalar(
                    out=stat[:, w, 2:4], in0=stat[:, w, 0:2], scalar1=-0.125, scalar2=None, op0=ALU.mult),
                    [R(("st", w, 0))], [R(("st", w, 1))])

            def st_exp():
                for hh in range(2):
                    simple("act", lambda e, hh=hh: e.activation(
                        out=sm[:, w, hh, :], in_=sm[:, w, hh, :], func=AF.Exp, bias=stat[:, w, 2 + hh:3 + hh],
                        scale=0.125, accum_out=stat[:, w, 4 + hh:5 + hh]),
                        [R(("sm", w, hh)), R(("smk", w)), R(("st", w, 1))],
                        [R(("sm", w, hh)), R(("smk", w)), R(("st", w, 2 + hh))])

            def st_sinkadd():
                pass

            def st_sinkexp():
                pass

            def st_den():
                simple("dve", lambda e: e.reciprocal(out=stat[:, w, 6:8], in_=stat[:, w, 4:6]),
                       [R(("st", w, 2)), R(("st", w, 3))], [R(("st", w, 7))])

            def st_scale():
                simple("dve", lambda e: e.tensor_scalar(
                    out=pp[:, w, 0, :], in0=sm[:, w, 0, 0:256], scalar1=stat[:, w, 6:7], scalar2=None, op0=ALU.mult),
                    [R(("sm", w, 0)), R(("st", w, 7))], [R(("pp", w, 0))])
                simple("dve", lambda e: e.tensor_scalar(
                    out=pp[:, w, 1, :], in0=sm[:, w, 1, 0:256], scalar1=stat[:, w, 7:8], scalar2=None, op0=ALU.mult),
                    [R(("sm", w, 1)), R(("st", w, 7))], [R(("pp", w, 1))])

            def st_T():
                def tfn(e):
                    ins = None
                    for hh in range(2):
                        for half in range(2):
                            k0 = (hh * 2 + half) * 128
                            ins = e.transpose(psTv[:, k0:k0 + 128], pp[:, w, hh, half * 128:half * 128 + 128], ident[:])
                    return ins
                P.add("pe", tfn, reads=[R(("pp", w, 0)), R(("pp", w, 1)), R("ident")], writes=[rT])

            def st_Tcopy():
                simple("act", lambda e: e.activation(out=pT[:, w, :, :], in_=psTv[:, :], func=AF.Copy),
                       [rT], [R(("pT", w))])

            def st_PV():
                def pvfn(e):
                    ins = None
                    for hh in range(2):
                        for half in range(2):
                            ins = e.matmul(ps[64 * hh:64 * hh + 64, bS[1], 256:384],
                                           lhsT=vbuf[:, i + half, g * 64:g * 64 + 64],
                                           rhs=pT[:, w, hh * 2 + half, :],
                                           start=(half == 0), stop=(half == 1), tile_position=(0, 64 * hh),
                                           skip_group_check=True)
                    return ins
                P.add("pe", pvfn, reads=[R(("pT", w)), R(("v", i)), R(("v", i + 1))], writes=[rO])

            def st_Ocopy():
                simple("act", lambda e: e.activation(
                    out=G[:, ch, 128 * i:128 * i + 128], in_=ps[:, bS[1], 256:384], func=AF.Copy),
                    [rO], [R(("G", ch), 128 * i, 128 * i + 128)])

            def st_maxnegm():
                st_max()
                st_negm()

            nop = lambda: None
            return [st_scores, st_mask, st_max, st_negm, st_exp, nop, st_den, st_scale, nop, st_T, st_Tcopy,
                    nop, nop, nop, st_PV, st_Ocopy]

        def l1_mixer(tile_t, T, o, acc):
            ka = 2 if tile_t == 0 else 0
            kc_base = 0 if tile_t == 0 else 128
            def ld_tab(e):
                e.dma_start(out=cosT[:], in_=cosd[:, 512 * tile_t: 512 * tile_t + KCOLS]).then_inc(ds_tab.sem, 16)
                return e.dma_start(out=sinT[:], in_=sind[:, 512 * tile_t: 512 * tile_t + KCOLS]).then_inc(ds_tab.sem, 16)
            P.add("sp", ld_tab, writes=[R("tab", 0, KCOLS)], dsem=ds_tab, ndma=2)
            if tile_t > 0:
                simple("act", lambda e: e.activation(out=kbuf[:, :, 0:128], in_=kbuf[:, :, 512:640], func=AF.Copy),
                       [R(("k", c), 512, 640) for c in range(8)], [R(("k", c), 0, 128) for c in range(8)])
                simple("act", lambda e: e.activation(out=vbuf[:, 0, :], in_=vbuf[:, 4, :], func=AF.Copy),
                       [R(("v", 4))], [R(("v", 0))])
            for g in range(4):
                slot, wres = wnext(tile_t, ("k", g))
                for (sa, sb_) in subtiles(ka, T):
                    n = sb_ - sa
                    kc0 = sa - ka + kc_base
                    prs = [[(wring[:, slot, kc * 256 + half * 128: kc * 256 + half * 128 + 128], xn[:, kc, sa:sb_])
                            for kc in range(NCH)] for half in range(2)]
                    if g == 0:
                        bks = mm_staggered([(n, prs[half], [[R(("xn", kc), sa, sb_)] for kc in range(NCH)], [wres])
                                            for half in range(2)])
                    else:
                        bks = []
                        for half in range(2):
                            bk = bank()
                            bks.append(bk)
                            mm_group(ps[:, bk, 0:n], prs[half], [wres] + [R(("xn", kc), sa, sb_) for kc in range(NCH)],
                                     [R(("ps", bk))])
                    rope_pair(bks[0], bks[1], n, kc0, kbuf[:, 2 * g, kc0:kc0 + n], kbuf[:, 2 * g + 1, kc0:kc0 + n],
                              R(("k", 2 * g), kc0, kc0 + n), R(("k", 2 * g + 1), kc0, kc0 + n))
            slot, wres = wnext(tile_t, ("v",))
            for bb in (range(5) if tile_t == 0 else range(1, 5)):
                lt = (2 + 128 * bb) if tile_t == 0 else 128 * (bb - 1)
                bk = bank()
                pairs = [(xn[:, kc, lt:lt + 128], wring[:, slot, kc * 256: kc * 256 + 256]) for kc in range(NCH)]
                mm_group(ps[:, bk, 0:256], pairs, [wres] + [R(("xn", kc), lt, lt + 128) for kc in range(NCH)],
                         [R(("ps", bk))])
                simple("act", lambda e, bk=bk, bb=bb: e.activation(out=vbuf[:, bb, :], in_=ps[:, bk, 0:256], func=AF.Copy),
                       [R(("ps", bk))], [R(("v", bb))])
            for j in range(8):
                slot, wres = wnext(tile_t, ("q", j))
                bks = []
                for half in range(2):
                    bk = bank()
                    bks.append(bk)
                    pairs = [(wring[:, slot, kc * 256 + half * 128: kc * 256 + half * 128 + 128], xn[:, kc, o:o + TQ])
                             for kc in range(NCH)]
                    mm_group(ps[:, bk, 0:TQ], pairs, [wres] + [R(("xn", kc), o, o + TQ) for kc in range(NCH)],
                             [R(("ps", bk))])
                rope_pair(bks[0], bks[1], TQ, 128, qbuf[:, 2 * j, :], qbuf[:, 2 * j + 1, :],
                          R(("q", 2 * j)), R(("q", 2 * j + 1)))
            steps = []
            sidx = 0
            for i in range(4):
                mk = 0 if (tile_t == 0 and i == 0) else 1
                for j in range(8):
                    for pr in range(2):
                        steps.append(att_step(sidx, i, j, pr, mk))
                        sidx += 1
            NS = len(steps[0])
            DLT = -(-NS // AW)
            for tau in range((len(steps) - 1) * DLT + NS):
                for s_i in range(max(0, (tau - NS) // DLT), min(len(steps), tau // DLT + 1)):
                    k = tau - s_i * DLT
                    if 0 <= k < NS:
                        steps[s_i][k]()
            for i in range(8):
                slot, wres = wnext(tile_t, ("wo", i))
                for cc in range(2):
                    m2 = 2 * i + cc
                    bk = bank()
                    pairs = [(wring[:, slot, kc * 256 + cc * 128: kc * 256 + cc * 128 + 128], G[:, kc, 0:TQ])
                             for kc in range(NCH)]
                    mm_group(ps[:, bk, 0:TQ], pairs, [wres] + [R(("G", kc), 0, TQ) for kc in range(NCH)], [R(("ps", bk))])
                    simple("dve", lambda e, bk=bk, m2=m2: e.tensor_tensor(
                        out=h[:, m2, o:o + TQ], in0=ps[:, bk, 0:TQ], in1=h[:, m2, o:o + TQ], op=ALU.add),
                        [R(("ps", bk)), R(("h", m2), o, o + TQ)], [R(("h", m2), o, o + TQ)])
                    acc.flush(keep=1)
                    acc.feed(m2)

        ds_xp = DSem(sem("dxp"))
        stage_ap = []
        stage_res = []
        for c in range(8):
            stage_ap.append(qbuf[:, 2 * c:2 * c + 2, :].rearrange("p a b -> p (a b)").bitcast(F32))
            stage_res.append([R(("q", 2 * c)), R(("q", 2 * c + 1))])
        g_hi = G[:, 8:16, :].rearrange("p a b -> p (a b)").bitcast(F32)
        for k5 in range(5):
            stage_ap.append(g_hi[:, k5 * TQ:(k5 + 1) * TQ])
            stage_res.append([R(("G", ch), 0, TMAX) for ch in range(8, 16)])
        sm_flat = sm[:, :, :, :].rearrange("p a b c -> p (a b c)")
        for k3 in range(3):
            stage_ap.append(sm_flat[:, k3 * TQ:(k3 + 1) * TQ])
            stage_res.append([R(("sm", w4, h2)) for w4 in range(AW) for h2 in range(2)] + [R(("smk", w4)) for w4 in range(AW)])

        def prefetch_x(next_t):
            c0 = HALO + TQ * next_t

            def fn(e):
                ins = None
                for c in range(NCH):
                    ins = e.dma_start(out=stage_ap[c], in_=xT[c, :, c0:c0 + TQ]).then_inc(ds_xp.sem, 16)
                return ins
            P.add("sp", fn, writes=[r for rs in stage_res for r in rs], dsem=ds_xp, ndma=NCH)

        def unstage_x():
            for c in range(NCH):
                simple("act", lambda e, c=c: e.activation(out=h[:, c, 0:TQ], in_=stage_ap[c], func=AF.Copy),
                       stage_res[c], [R(("h", c), 0, TQ)])

        octr = [0]

        ds_cp = [DSem(sem(f"dcp{c}")) for c in range(NCH)]

        def write_out(tile_t, o, normed, refill=False):
            thunks = []
            order = [8, 9, 10, 11, 12] + list(range(8)) + [13, 14, 15] if refill else list(range(NCH))
            for c in order:
                def th(c=c):
                    oi = octr[0] % 2
                    octr[0] += 1
                    if normed:
                        simple("dve", lambda e: e.scalar_tensor_tensor(
                            out=scr[:, 2 + oi, :], in0=h[:, c, o:o + TQ], scalar=gain(4, c), in1=rstd[:, o:o + TQ],
                            op0=ALU.mult, op1=ALU.mult),
                            [R(("h", c), o, o + TQ), R("rstd", o, o + TQ), R("cst")], [R(("scr", 2 + oi))])
                    else:
                        simple("dve", lambda e: e.tensor_copy(out=scr[:, 2 + oi, :], in_=h[:, c, o:o + TQ]),
                               [R(("h", c), o, o + TQ)], [R(("scr", 2 + oi))])
                    P.add("sp", lambda e: e.dma_start(
                        out=outT[c, :, TQ * tile_t: TQ * tile_t + TQ], in_=scr[:, 2 + oi, :]).then_inc(ds_out[oi].sem, 16),
                        reads=[R(("scr", 2 + oi))], writes=[R(("out", tile_t, c))], dsem=ds_out[oi], ndma=1)
                    if refill:
                        P.add("sp", lambda e: e.dma_start(out=h[:, c, 0:TQ], in_=stage_ap[c]).then_inc(ds_cp[c].sem, 16),
                              reads=stage_res[c], writes=[R(("h", c), 0, TQ)], dsem=ds_cp[c], ndma=1)
                thunks.append(th)
            return thunks

        pending_out = None
        for t in range(NT_TILES):
            P.epoch = t
            T = TMAX if t == 0 else TQ
            o = HALO if t == 0 else 0
            c0 = 0 if t == 0 else HALO + TQ * t

            def ldx(e, T=T, c0=c0):
                ins = None
                for c in range(NCH):
                    ins = e.dma_start(out=h[:, c, 0:T], in_=xT[c, :, c0:c0 + T]).then_inc(ds_x.sem, 16)
                return ins
            ka = 2 if t == 0 else 0
            if t == 0 or dbg:
                P.add("sp", ldx, writes=[R(("h", c), 0, T) for c in range(NCH)], dsem=ds_x, ndma=NCH)
                acc = StatAcc(0, T)
                for c in range(NCH):
                    acc.feed(c)
                    acc.flush(keep=1)
                acc.finish()
                norm_apply_xn(0, 0, T)
            acc = StatAcc(0, T)
            l0_mixer(t, T, acc, side=pending_out)
            pending_out = None
            acc.finish()
            if dbg == "l0mix":
                wctr[0] = (t + 1) * NWT
                for th in write_out(t, o, False):
                    th()
                continue
            norm_apply_xn(1, 0, T)
            acc = StatAcc(ka, T)
            mlp(t, 0, T, 0, acc)
            acc.finish()
            if dbg == "l0":
                wctr[0] = (t + 1) * NWT
                for th in write_out(t, o, False):
                    th()
                continue
            norm_apply_xn(2, ka, T)
            acc = StatAcc(o, o + TQ)
            l1_mixer(t, T, o, acc)
            acc.finish()
            if dbg == "l1mix":
                wctr[0] = (t + 1) * NWT
                for th in write_out(t, o, False):
                    th()
                continue
            norm_apply_xn(3, o, o + TQ)
            hooks = None
            accN = None
            if t + 1 < NT_TILES:
                prefetch_x(t + 1)
                stage_src = lambda c: (stage_ap[c], stage_res[c])
                accN = StatAcc(0, TQ, banks=[7], src=stage_src, rdst=(scr[:, 0, :], ("scr", 0)))

                def early_stats(accN=accN):
                    for c in range(NCH):
                        accN.feed(c)
                        accN.flush(keep=1)
                    accN.flush(0)
                hooks = {12: early_stats}
            acc = StatAcc(o, o + TQ)
            mlp(t, o, o + TQ, 1, acc, hooks)
            acc.finish()
            assert wctr[0] == (t + 1) * NWT, (wctr[0], t)
            if accN is not None:
                accN.finish()
                norm_apply_xn(0, 0, TQ, src=stage_src, rs=(scr[:, 0, :], ("scr", 0)))
                pending_out = write_out(t, o, True, refill=True)
            else:
                for th in write_out(t, o, True):
                    th()
        P.epoch = NT_TILES
        P.add("sp", lambda e: None, reads=[R(("out", t, c)) for t in range(NT_TILES) for c in range(NCH)])
        P.finalize_and_emit(nc, sems)
    return nc, PLAN


def _tile16(W, cols):
    sub = W[:, cols]
    n = sub.shape[1]
    return sub.reshape(16, 128, n).transpose(1, 0, 2).reshape(128, 16 * n)


def build_weight_stream(plan, w_in_conv, w_out_conv, w_up_0, w_down_0, w_qkv, w_o, w_up_1, w_down_1):
    wts = np.zeros((len(plan), 128, SLOT), np.float32)
    ar = np.arange
    ups = [w_up_0, w_up_1]
    downs = [w_down_0, w_down_1]
    win_cache = {}
    for n, key in enumerate(plan):
        kind = key[0]
        if kind == "win":
            m, half = key[1], key[2]
            if m not in win_cache:
                win_cache.clear()
                cols = np.concatenate([m * 128 + ar(128), 2048 + m * 128 + ar(128), 4096 + m * 128 + ar(128)])
                win_cache[m] = w_in_conv[:, cols].reshape(16, 128, 384)
            sub = win_cache[m]
            wts[n, :, 0:3072] = sub[half * 8:(half + 1) * 8].transpose(1, 0, 2).reshape(128, 3072)
        elif kind == "wout":
            wts[n] = _tile16(w_out_conv, key[1] * 256 + ar(256))
        elif kind == "up":
            _, lyr, blk, ui = key
            wts[n] = _tile16(ups[lyr], blk * 512 + ui * 256 + ar(256))
        elif kind == "down":
            _, lyr, blk, di = key
            r0 = (blk * 4 + di * 2) * 128
            wts[n] = downs[lyr][r0:r0 + 256].reshape(2, 128, 2048).transpose(1, 0, 2).reshape(128, 4096)
        elif kind == "k":
            g = key[1]
            a = 2048 + g * 64 + np.tile(ar(32), 4)
            b = 2048 + g * 64 + 32 + np.tile(ar(32), 4)
            wts[n] = _tile16(w_qkv, np.concatenate([a, b]))
        elif kind == "v":
            wts[n] = _tile16(w_qkv, 2304 + ar(256))
        elif kind == "q":
            j = key[1]
            a = np.concatenate([(4 * j + s) * 64 + ar(32) for s in range(4)])
            b = np.concatenate([(4 * j + s) * 64 + 32 + ar(32) for s in range(4)])
            wts[n] = _tile16(w_qkv, np.concatenate([a, b]))
        elif kind == "wo":
            wts[n] = _tile16(w_o, key[1] * 256 + ar(256))
        else:
            raise ValueError(key)
    return wts


_NC_CACHE = {}


def kernel(x, meta_tokens, norm_mix_0, w_in_conv, conv_w, w_out_conv, norm_mlp_0, w_up_0, w_down_0,
           norm_mix_1, w_qkv, attn_sinks, w_o, norm_mlp_1, w_up_1, w_down_1, norm_final, _dbg=""):
    f = lambda a: np.asarray(a, dtype=np.float32)
    x = f(x)
    B = x.shape[0]
    if _dbg not in _NC_CACHE:
        _NC_CACHE[_dbg] = build_program(_dbg)
    nc, plan = _NC_CACHE[_dbg]
    assert len(plan) == NWT or _dbg
    wts = build_weight_stream(plan, f(w_in_conv), f(w_out_conv), f(w_up_0), f(w_down_0), f(w_qkv), f(w_o), f(w_up_1),
                              f(w_down_1))
    if wts.shape[0] < NWT:
        wts = np.concatenate([wts, np.zeros((NWT - wts.shape[0], 128, SLOT), np.float32)], axis=0)
    cst = np.zeros((128, 160), np.float32)
    for n_, gvec in enumerate([norm_mix_0, norm_mlp_0, norm_mix_1, norm_mlp_1, norm_final]):
        cst[:, n_ * 16:(n_ + 1) * 16] = f(gvec).reshape(16, 128).T
    for k in range(3):
        cst[:, 80 + k * 16: 80 + (k + 1) * 16] = f(conv_w)[k].reshape(16, 128).T
    cst[:, 128:160] = f(attn_sinks)[None, :]
    ident = np.eye(128, dtype=np.float32).astype(ml_dtypes.bfloat16)
    inv = (np.float32(10000.0) ** (-np.arange(0, 64, 2, dtype=np.float32) / np.float32(64))).astype(np.float32)
    sgn = np.ones((128, 1), np.float32)
    in_maps = []
    for c in range(8):
        b, j = divmod(c, 4)
        lo = j * TOK_CORE
        hfull = np.concatenate([np.zeros((114, D), np.float32), f(meta_tokens), x[b]], axis=0)
        xTc = np.ascontiguousarray(hfull[lo:lo + XCOLS].T).reshape(NCH, 128, XCOLS)
        pos = (np.arange(2176, dtype=np.float32) + np.float32(lo - 112)).astype(np.float32)
        ang = (pos[None, :] * inv[:, None]).astype(np.float32)
        cosd = np.tile(np.cos(ang).astype(np.float32), (4, 1))
        sind = np.tile(np.sin(ang).astype(np.float32), (4, 1))
        qq = np.arange(128)[:, None]
        kk = np.arange(256)[None, :]
        allowed = (kk > qq) & (kk <= qq + 128)
        m1 = np.where(allowed, 0.0, NEG).astype(np.float32)
        m0 = np.where(allowed & ((kk >= 112) | (j != 0)), 0.0, NEG).astype(np.float32)
        maskd = np.ascontiguousarray(np.stack([m0, m1], axis=1))
        in_maps.append({"xT": xTc, "wts": wts, "cosd": np.ascontiguousarray(cosd), "sind": np.ascontiguousarray(sind),
                        "maskd": maskd, "cstd": cst, "identd": ident})
    res = run_bass_kernel_spmd(nc, in_maps, core_ids=list(range(8)))
    out = np.empty((B, SEQ, D), np.float32)
    for c in range(8):
        b, j = divmod(c, 4)
        oT = res.results[c]["outT"].reshape(D, TOK_CORE)
        out[b, j * TOK_CORE:(j + 1) * TOK_CORE, :] = oT.T
    return out
```

```python
import os
import numpy as np
import ml_dtypes
import concourse.bass as bass
import concourse.mybir as mybir
from concourse.bass_utils import run_bass_kernel_spmd

F32 = mybir.dt.float32
BF16 = mybir.dt.bfloat16
ALU = mybir.AluOpType
AF = mybir.ActivationFunctionType
AX = mybir.AxisListType

D = 2048
NCH = 16
DFF = 8192
SEQ = 8192
NMETA = 16
HALO = 130
TOK_CORE = 2048
NT_TILES = 4
TQ = 512
TMAX = HALO + TQ
XCOLS = TOK_CORE + HALO
KCOLS = 640
RING = 5
SLOT = 4096
EPS = 1e-5
NEG = -1e30
NWT = 32 + 8 + 64 + 18 + 64


class DSem:
    def __init__(self, sem):
        self.sem = sem
        self.count = 0


class Op:
    __slots__ = ("eng", "fn", "raw", "other", "signal", "is_dma", "dsem", "dval", "epoch", "sval", "ndma")

    def __init__(self, eng, fn):
        self.eng = eng
        self.fn = fn
        self.raw = []
        self.other = []
        self.signal = False
        self.is_dma = False
        self.dsem = None
        self.dval = 0
        self.epoch = 0
        self.sval = 0
        self.ndma = 0


class Prog:
    ENGS = ("pe", "act", "dve", "pool", "sp")

    def __init__(self):
        self.q = {e: [] for e in self.ENGS}
        self.wr = {}
        self.rd = {}
        self.epoch = 0

    def add(self, eng, fn, reads=(), writes=(), dsem=None, ndma=0):
        op = Op(eng, fn)
        op.epoch = self.epoch
        if dsem is not None:
            op.is_dma = True
            op.dsem = dsem
            op.ndma = ndma
            dsem.count += 16 * ndma
            op.dval = dsem.count
        raw, other = [], []
        psr = [r for r in reads if isinstance(r[0], tuple) and r[0][0] == "ps"]
        if psr:
            writes = list(writes) + [r for r in psr if r not in writes]
        for (k, a, b) in reads:
            for (wa, wb, w) in self.wr.get(k, ()):
                if wa < b and a < wb:
                    raw.append(w)
        for (k, a, b) in writes:
            for (wa, wb, w) in self.wr.get(k, ()):
                if wa < b and a < wb:
                    other.append(w)
            for (ra, rb, r) in self.rd.get(k, ()):
                if ra < b and a < rb:
                    other.append(r)
        op.raw = [d for d in dict.fromkeys(raw) if d is not op]
        op.other = [d for d in dict.fromkeys(other) if d is not op]
        for (k, a, b) in reads:
            self.rd.setdefault(k, []).append((a, b, op))
        for (k, a, b) in writes:
            self.wr[k] = [x for x in self.wr.get(k, ()) if not (a <= x[0] and x[1] <= b)] + [(a, b, op)]
            self.rd[k] = [x for x in self.rd.get(k, ()) if not (a <= x[0] and x[1] <= b)]
        self.q[eng].append(op)
        return op

    def finalize_and_emit(self, nc, sems):
        for e in self.ENGS:
            for op in self.q[e]:
                deps = []
                for d in op.raw:
                    if d.is_dma or d.eng != op.eng or op.eng != "pe":
                        deps.append(d)
                for d in op.other:
                    if d.is_dma or d.eng != op.eng or op.eng != "pe":
                        deps.append(d)
                op.raw = deps
                op.other = []
                for d in deps:
                    if not d.is_dma:
                        d.signal = True
        for e in self.ENGS:
            cnt = {}
            for op in self.q[e]:
                if op.signal and not op.is_dma:
                    cnt[op.epoch] = cnt.get(op.epoch, 0) + 1
                    op.sval = cnt[op.epoch]
        handles = {"pe": nc.tensor, "act": nc.scalar, "dve": nc.vector, "pool": nc.gpsimd, "sp": nc.sync}

        def emit(ename, e):
            waited = {}
            for op in self.q[ename]:
                for d in op.raw:
                    if d.is_dma:
                        s, v = d.dsem.sem, d.dval
                    else:
                        s, v = sems[(d.eng, d.epoch)], d.sval
                    key = id(s)
                    if waited.get(key, 0) < v:
                        e.wait_ge(s, v)
                        waited[key] = v
                ins = op.fn(e)
                if op.signal and not op.is_dma:
                    ins.then_inc(sems[(op.eng, op.epoch)], 1)

        with nc.Block() as block:
            @block.tensor
            def _(e):
                emit("pe", e)

            @block.scalar
            def _(e):
                emit("act", e)

            @block.vector
            def _(e):
                emit("dve", e)

            @block.gpsimd
            def _(e):
                emit("pool", e)

            @block.sync
            def _(e):
                emit("sp", e)


def subtiles(a, b, n=512):
    out = []
    while a < b:
        out.append((a, min(a + n, b)))
        a += n
    return out


def build_program(dbg=""):
    nc = bass.Bass("TRN2", target_bir_lowering=False)
    xT = nc.dram_tensor("xT", [NCH, 128, XCOLS], F32, kind="ExternalInput").ap()
    wts = nc.dram_tensor("wts", [NWT, 128, SLOT], F32, kind="ExternalInput").ap()
    cosd = nc.dram_tensor("cosd", [128, 2176], F32, kind="ExternalInput").ap()
    sind = nc.dram_tensor("sind", [128, 2176], F32, kind="ExternalInput").ap()
    maskd = nc.dram_tensor("maskd", [128, 2, 256], F32, kind="ExternalInput").ap()
    cstd = nc.dram_tensor("cstd", [128, 160], F32, kind="ExternalInput").ap()
    identd = nc.dram_tensor("identd", [128, 128], BF16, kind="ExternalInput").ap()
    outT = nc.dram_tensor("outT", [NCH, 128, TOK_CORE], F32, kind="ExternalOutput").ap()

    from contextlib import ExitStack
    es = ExitStack()
    with es:
        def sb(name, shape, dt):
            return es.enter_context(nc.sbuf_tensor(name, shape, dt))

        def sem(name):
            return es.enter_context(nc.semaphore(name))

        h = sb("h", [128, NCH, TMAX], F32)
        xn = sb("xn", [128, NCH, TMAX], BF16)
        G = sb("G", [128, NCH, TMAX], BF16)
        kbuf = sb("kbuf", [128, 8, KCOLS], BF16)
        vbuf = sb("vbuf", [128, 5, 256], BF16)
        qbuf = sb("qbuf", [128, 16, TQ], BF16)
        cosT = sb("cosT", [128, KCOLS], F32)
        sinT = sb("sinT", [128, KCOLS], F32)
        masks = sb("masks", [128, 2, 256], F32)
        cst = sb("cst", [128, 160], F32)
        negsink = sb("negsink", [128, 32], F32)
        ident = sb("ident", [128, 128], BF16)
        ones = sb("ones", [128, 128], BF16)
        epsc = sb("epsc", [128, 1], F32)
        wring = sb("wring", [128, RING, SLOT], BF16)
        NSQ = 3
        sq = sb("sq", [128, NSQ, TMAX], BF16)
        rstd = sb("rstd", [128, TMAX], F32)
        vtail = sb("vtail", [128, NCH, 2], F32)
        b_sb = sb("b_sb", [128, 2, TMAX], F32)
        c_sb = sb("c_sb", [128, 2, TMAX], F32)
        v_sb = sb("v_sb", [128, 2, TMAX + 2], F32)
        t_sb = sb("t_sb", [128, 2, TMAX], F32)
        scr = sb("scr", [128, 4, 512], F32)
        AW = 4
        sm = sb("sm", [128, AW, 2, 257], F32)
        pp = sb("pp", [128, AW, 2, 256], BF16)
        pT = sb("pT", [128, AW, 4, 128], BF16)
        stat = sb("stat", [128, AW, 8], F32)

        ps = es.enter_context(nc.psum_tensor("ps", [128, 8, 512], F32))

        P = Prog()
        sems = {}
        for e in Prog.ENGS:
            for ep in range(NT_TILES + 1):
                sems[(e, ep)] = sem(f"s_{e}_{ep}")
        ds_w = [DSem(sem(f"dw{i}")) for i in range(RING)]
        ds_x = DSem(sem("dx"))
        ds_tab = DSem(sem("dtab"))
        ds_krep = DSem(sem("dkrep"))
        ds_c = DSem(sem("dc"))
        ds_out = [DSem(sem(f"do{i}")) for i in range(2)]

        gain = lambda n, c: cst[:, n * 16 + c: n * 16 + c + 1]
        convw = lambda k, c: cst[:, 80 + k * 16 + c: 80 + k * 16 + c + 1]
        sinks = cst[:, 128:160]

        bank_ctr = [0]

        def bank():
            b = bank_ctr[0] % 6
            bank_ctr[0] += 1
            return b

        def R(key, a=0, b=1):
            if isinstance(key, tuple) and key[0] == "ps":
                return (key, 0, 512)
            return (key, a, b)

        wctr = [0]
        PLAN = []

        def wnext(tile_t, key, ncols_used=SLOT):
            n = wctr[0]
            wctr[0] += 1
            slot = n % RING
            widx = n % NWT
            if tile_t == 0:
                assert len(PLAN) == widx
                PLAN.append(key)
            else:
                assert PLAN[widx] == key, (PLAN[widx], key)
            res = R(("w", slot))

            def fn(e, slot=slot, widx=widx, nc_=ncols_used):
                return e.dma_start(out=wring[:, slot, 0:nc_], in_=wts[widx, :, 0:nc_]).then_inc(ds_w[slot].sem, 16)
            P.add("pool", fn, writes=[res], dsem=ds_w[slot], ndma=1)
            return slot, res

        def mm_group(out_ap, pairs, reads, writes, **kw):
            def fn(e):
                n = len(pairs)
                ins = None
                for i, (l, r) in enumerate(pairs):
                    ins = e.matmul(out_ap, lhsT=l, rhs=r, start=(i == 0), stop=(i == n - 1), **kw)
                return ins
            return P.add("pe", fn, reads=reads, writes=writes)

        def simple(eng, fn, reads, writes):
            return P.add(eng, fn, reads=reads, writes=writes)

        def mm_staggered(descs):
            banks = [bank() for _ in descs]
            K = len(descs[0][1])
            for k in range(K):
                def fn(e, k=k):
                    ins = None
                    for (n, pairs, pr, cr), bk in zip(descs, banks):
                        ins = e.matmul(ps[:, bk, 0:n], lhsT=pairs[k][0], rhs=pairs[k][1], start=(k == 0), stop=(k == K - 1))
                    return ins
                reads = []
                for (n, pairs, pr, cr) in descs:
                    reads += list(pr[k]) + list(cr)
                P.add("pe", fn, reads=list(dict.fromkeys(reads)), writes=[R(("ps", bk)) for bk in banks])
            return banks

        def ld_consts(e):
            e.dma_start(out=cst[:], in_=cstd[:, :]).then_inc(ds_c.sem, 16)
            e.dma_start(out=masks[:], in_=maskd[:, :, :]).then_inc(ds_c.sem, 16)
            return e.dma_start(out=ident[:], in_=identd[:, :]).then_inc(ds_c.sem, 16)
        P.add("sp", ld_consts, writes=[R("cst"), R("masks"), R("ident")], dsem=ds_c, ndma=3)
        simple("dve", lambda e: e.memset(ones[:], 1.0), [], [R("ones")])
        simple("dve", lambda e: e.memset(epsc[:], float(EPS)), [], [R("epsc")])
        simple("dve", lambda e: e.memset(vtail[:], 0.0), [], [R(("vtail", c)) for c in range(NCH)])
        simple("dve", lambda e: e.tensor_scalar(out=negsink[:], in0=sinks, scalar1=8.0, scalar2=None, op0=ALU.mult),
               [R("cst")], [R("negsink")])

        class StatAcc:
            def __init__(self, a, b, banks=None, src=None, rdst=None):
                self.a, self.b = a, b
                self.subs = subtiles(a, b)
                self.banks = banks if banks is not None else [6, 7][:len(self.subs)]
                self.src = src
                self.rdst = rdst
                self.n = 0
                self.pending = []

            def feed(self, c, eng="act"):
                a, b = self.a, self.b
                idx = self.n
                slot = idx % NSQ
                self.n += 1
                if self.src is not None:
                    sap, sres = self.src(c)
                else:
                    sap, sres = h[:, c, a:b], [R(("h", c), a, b)]
                if eng == "act":
                    simple("act", lambda e: e.activation(out=sq[:, slot, a:b], in_=sap, func=AF.Square),
                           sres, [R(("sq", slot), a, b)])
                else:
                    simple("dve", lambda e: e.tensor_tensor(out=sq[:, slot, a:b], in0=sap, in1=sap, op=ALU.mult),
                           sres, [R(("sq", slot), a, b)])
                subs, banks = self.subs, self.banks

                def pe_thunk():
                    def fn(e):
                        ins = None
                        for (sa, sb_), bk in zip(subs, banks):
                            ins = e.matmul(ps[:, bk, 0:sb_ - sa], lhsT=ones[:], rhs=sq[:, slot, sa:sb_],
                                           start=(idx == 0), stop=(idx == NCH - 1))
                        return ins
                    P.add("pe", fn, reads=[R(("sq", slot), a, b), R("ones")], writes=[R(("ps", bk)) for bk in banks])
                self.pending.append(pe_thunk)

            def flush(self, keep=0):
                while len(self.pending) > keep:
                    self.pending.pop(0)()

            def finish(self):
                assert self.n == NCH
                self.flush(0)
                rap, rkey = self.rdst if self.rdst is not None else (rstd, "rstd")
                for (sa, sb_), bk in zip(self.subs, self.banks):
                    simple("act", lambda e, sa=sa, sb_=sb_, bk=bk: e.activation(
                        out=rap[:, sa:sb_], in_=ps[:, bk, 0:sb_ - sa], func=AF.Ln, bias=epsc[:, 0:1],
                        scale=float(1.0 / D)),
                        [R(("ps", bk)), R("epsc")], [R(rkey, sa, sb_)])
                    simple("act", lambda e, sa=sa, sb_=sb_: e.activation(
                        out=rap[:, sa:sb_], in_=rap[:, sa:sb_], func=AF.Exp, scale=-0.5),
                        [R(rkey, sa, sb_)], [R(rkey, sa, sb_)])

        def norm_apply_xn(nidx, a, b, src=None, rs=None):
            rap, rkey = rs if rs is not None else (rstd, "rstd")
            for c in range(NCH):
                if src is not None:
                    sap, sres = src(c)
                else:
                    sap, sres = h[:, c, a:b], [R(("h", c), a, b)]
                simple("dve",
                       lambda e, c=c, sap=sap: e.scalar_tensor_tensor(out=xn[:, c, a:b], in0=sap, scalar=gain(nidx, c),
                                                                      in1=rap[:, a:b], op0=ALU.mult, op1=ALU.mult),
                       list(sres) + [R(rkey, a, b), R("cst")], [R(("xn", c), a, b)])

        def mlp(tile_t, a, b, lyr, acc, hooks=None):
            subs = subtiles(a, b)
            rctr = [0]

            def up(blk):
                bi = blk % 2
                wu = [wnext(tile_t, ("up", lyr, blk, 0)), wnext(tile_t, ("up", lyr, blk, 1))]
                groups = []
                for j in range(4):
                    slot, wres = wu[j // 2]
                    coff = (j % 2) * 128
                    for (sa, sb_) in subs:
                        pairs = [(wring[:, slot, kc * 256 + coff: kc * 256 + coff + 128], xn[:, kc, sa:sb_])
                                 for kc in range(NCH)]
                        groups.append((j, sa, sb_, pairs, wres))
                nst = min(len(groups), 6) if blk == 0 else 0
                banks = {}
                if nst:
                    bl = mm_staggered([(sb_ - sa, pairs, [[R(("xn", kc), sa, sb_)] for kc in range(NCH)], [wres])
                                       for (j, sa, sb_, pairs, wres) in groups[:nst]])
                    for gi in range(nst):
                        banks[gi] = bl[gi]
                for gi, (j, sa, sb_, pairs, wres) in enumerate(groups):
                    n = sb_ - sa
                    if gi in banks:
                        bk = banks[gi]
                    else:
                        bk = bank()
                        mm_group(ps[:, bk, 0:n], pairs,
                                 [wres] + [R(("xn", kc), sa, sb_) for kc in range(NCH)], [R(("ps", bk))])
                    ri = rctr[0] % 2
                    rctr[0] += 1
                    simple("act",
                           lambda e, bk=bk, n=n, ri=ri: e.activation(out=scr[:, ri, 0:n], in_=ps[:, bk, 0:n], func=AF.Relu),
                           [R(("ps", bk))], [R(("scr", ri))])
                    simple("act",
                           lambda e, n=n, ri=ri, bi=bi, j=j, sa=sa, sb_=sb_: e.activation(
                               out=G[:, bi * 4 + j, sa:sb_], in_=scr[:, ri, 0:n], func=AF.Square),
                           [R(("scr", ri))], [R(("G", bi * 4 + j), sa, sb_)])

            def down(blk):
                bi = blk % 2
                wd = [wnext(tile_t, ("down", lyr, blk, 0)), wnext(tile_t, ("down", lyr, blk, 1))]
                for m2 in range(NCH):
                    for (sa, sb_) in subs:
                        bk = bank()
                        n = sb_ - sa
                        pairs = [(wring[:, wd[j // 2][0], (j % 2) * 2048 + m2 * 128: (j % 2) * 2048 + m2 * 128 + 128],
                                  G[:, bi * 4 + j, sa:sb_]) for j in range(4)]
                        mm_group(ps[:, bk, 0:n], pairs,
                                 [wd[0][1], wd[1][1]] + [R(("G", bi * 4 + j), sa, sb_) for j in range(4)],
                                 [R(("ps", bk))])
                        simple("dve",
                               lambda e, bk=bk, n=n, m2=m2, sa=sa, sb_=sb_: e.tensor_tensor(
                                   out=h[:, m2, sa:sb_], in0=ps[:, bk, 0:n], in1=h[:, m2, sa:sb_], op=ALU.add),
                               [R(("ps", bk)), R(("h", m2), sa, sb_)], [R(("h", m2), sa, sb_)])
                        if blk == 15:
                            acc.flush(keep=2)
                    if blk == 15:
                        acc.feed(m2)

            up(0)
            for blk in range(16):
                if hooks and blk in hooks:
                    hooks[blk]()
                if blk + 1 < 16:
                    up(blk + 1)
                down(blk)

        def l0_mixer(tile_t, T, acc, side=None):
            subs = subtiles(0, T)
            for m in range(NCH):
                mi = m % 2
                wa = wnext(tile_t, ("win", m, 0), 3072)
                wb = wnext(tile_t, ("win", m, 1), 3072)

                def wl(kc, off):
                    slot = wa[0] if kc < 8 else wb[0]
                    kk = kc % 8
                    return wring[:, slot, kk * 384 + off: kk * 384 + off + 128]
                groups = []
                for kind, off in (("c", 128), ("u", 256), ("b", 0)):
                    for (sa, sb_) in subs:
                        pairs = [(wl(kc, off), xn[:, kc, sa:sb_]) for kc in range(NCH)]
                        groups.append((kind, sa, sb_, pairs))
                stag = None
                if m == 0:
                    stag = mm_staggered([(sb_ - sa, pairs, [[R(("xn", kc), sa, sb_)] for kc in range(NCH)], [wa[1], wb[1]])
                                         for (kind, sa, sb_, pairs) in groups])
                for gi, (kind, sa, sb_, pairs) in enumerate(groups):
                    if True:
                        n = sb_ - sa
                        if stag is not None:
                            bk = stag[gi]
                        else:
                            bk = bank()
                            mm_group(ps[:, bk, 0:n], pairs,
                                     [wa[1], wb[1]] + [R(("xn", kc), sa, sb_) for kc in range(NCH)], [R(("ps", bk))])
                        if kind == "c":
                            simple("act", lambda e, bk=bk, n=n, sa=sa, sb_=sb_, mi=mi: e.activation(
                                out=c_sb[:, mi, sa:sb_], in_=ps[:, bk, 0:n], func=AF.Copy),
                                [R(("ps", bk))], [R(("c_sb", mi), sa, sb_)])
                        elif kind == "b":
                            simple("act", lambda e, bk=bk, n=n, sa=sa, sb_=sb_, mi=mi: e.activation(
                                out=b_sb[:, mi, sa:sb_], in_=ps[:, bk, 0:n], func=AF.Copy),
                                [R(("ps", bk))], [R(("b_sb", mi), sa, sb_)])
                        else:
                            simple("dve", lambda e, bk=bk, n=n, sa=sa, sb_=sb_, mi=mi: e.tensor_tensor(
                                out=v_sb[:, mi, 2 + sa:2 + sb_], in0=ps[:, bk, 0:n], in1=c_sb[:, mi, sa:sb_], op=ALU.mult),
                                [R(("ps", bk)), R(("c_sb", mi), sa, sb_)], [R(("v_sb", mi), 2 + sa, 2 + sb_)])
                simple("dve", lambda e, mi=mi, m=m: e.tensor_copy(out=v_sb[:, mi, 0:2], in_=vtail[:, m, :]),
                       [R(("vtail", m))], [R(("v_sb", mi), 0, 2)])
                simple("dve", lambda e, mi=mi, m=m: e.tensor_scalar(
                    out=t_sb[:, mi, 0:T], in0=v_sb[:, mi, 2:2 + T], scalar1=convw(2, m), scalar2=None, op0=ALU.mult),
                    [R(("v_sb", mi), 2, 2 + T), R("cst")], [R(("t_sb", mi), 0, T)])
                simple("dve", lambda e, mi=mi, m=m: e.scalar_tensor_tensor(
                    out=t_sb[:, mi, 0:T], in0=v_sb[:, mi, 1:1 + T], scalar=convw(1, m), in1=t_sb[:, mi, 0:T],
                    op0=ALU.mult, op1=ALU.add),
                    [R(("v_sb", mi), 1, 1 + T), R(("t_sb", mi), 0, T), R("cst")], [R(("t_sb", mi), 0, T)])
                simple("dve", lambda e, mi=mi, m=m: e.scalar_tensor_tensor(
                    out=t_sb[:, mi, 0:T], in0=v_sb[:, mi, 0:T], scalar=convw(0, m), in1=t_sb[:, mi, 0:T],
                    op0=ALU.mult, op1=ALU.add),
                    [R(("v_sb", mi), 0, T), R(("t_sb", mi), 0, T), R("cst")], [R(("t_sb", mi), 0, T)])
                simple("dve", lambda e, mi=mi, m=m: e.tensor_copy(out=vtail[:, m, :], in_=v_sb[:, mi, T:T + 2]),
                       [R(("v_sb", mi), T, T + 2)], [R(("vtail", m))])
                simple("dve", lambda e, mi=mi, m=m: e.tensor_tensor(
                    out=G[:, m, 0:T], in0=b_sb[:, mi, 0:T], in1=t_sb[:, mi, 0:T], op=ALU.mult),
                    [R(("b_sb", mi), 0, T), R(("t_sb", mi), 0, T)], [R(("G", m), 0, T)])
                if side:
                    side.pop(0)()
            while side:
                side.pop(0)()
            for i in range(8):
                slot, wres = wnext(tile_t, ("wout", i))
                for cc in range(2):
                    m2 = 2 * i + cc
                    for (sa, sb_) in subs:
                        bk = bank()
                        n = sb_ - sa
                        pairs = [(wring[:, slot, kc * 256 + cc * 128: kc * 256 + cc * 128 + 128], G[:, kc, sa:sb_])
                                 for kc in range(NCH)]
                        mm_group(ps[:, bk, 0:n], pairs, [wres] + [R(("G", kc), sa, sb_) for kc in range(NCH)],
                                 [R(("ps", bk))])
                        simple("dve", lambda e, bk=bk, n=n, m2=m2, sa=sa, sb_=sb_: e.tensor_tensor(
                            out=h[:, m2, sa:sb_], in0=ps[:, bk, 0:n], in1=h[:, m2, sa:sb_], op=ALU.add),
                            [R(("ps", bk)), R(("h", m2), sa, sb_)], [R(("h", m2), sa, sb_)])
                        acc.flush(keep=1)
                    acc.feed(m2)

        def rope_pair(bkA, bkB, n, tc0, outA, outB, resA, resB):
            cs = cosT[:, tc0:tc0 + n]
            sn = sinT[:, tc0:tc0 + n]
            tabs = [R("tab", tc0, tc0 + n)]
            A = ps[:, bkA, 0:n]
            B = ps[:, bkB, 0:n]
            simple("dve", lambda e: e.tensor_tensor(out=scr[:, 0, 0:n], in0=A, in1=cs, op=ALU.mult),
                   [R(("ps", bkA))] + tabs, [R(("scr", 0))])
            simple("dve", lambda e: e.tensor_tensor(out=scr[:, 1, 0:n], in0=B, in1=sn, op=ALU.mult),
                   [R(("ps", bkB))] + tabs, [R(("scr", 1))])
            simple("dve", lambda e: e.tensor_tensor(out=scr[:, 2, 0:n], in0=B, in1=cs, op=ALU.mult),
                   [R(("ps", bkB))] + tabs, [R(("scr", 2))])
            simple("dve", lambda e: e.tensor_tensor(out=scr[:, 3, 0:n], in0=A, in1=sn, op=ALU.mult),
                   [R(("ps", bkA))] + tabs, [R(("scr", 3))])
            simple("dve", lambda e: e.tensor_tensor(out=outA, in0=scr[:, 0, 0:n], in1=scr[:, 1, 0:n], op=ALU.subtract),
                   [R(("scr", 0)), R(("scr", 1))], resA if isinstance(resA, list) else [resA])
            simple("dve", lambda e: e.tensor_tensor(out=outB, in0=scr[:, 2, 0:n], in1=scr[:, 3, 0:n], op=ALU.add),
                   [R(("scr", 2)), R(("scr", 3))], resB if isinstance(resB, list) else [resB])

        def att_step(sidx, i, j, pr, mk):
            w = sidx % AW
            g = j // 2
            bS = [2 * w, 2 * w + 1]
            osl = sidx % 4
            hcol = 4 * j + 2 * pr
            ch = 2 * j + pr
            psTv = ps[:, 2 * w, 256:512].bitcast(BF16)
            rS = [R(("ps", bS[0]), 0, 256), R(("ps", bS[1]), 0, 256)]
            rT = R(("ps", bS[0]))
            rO = R(("ps", bS[1]))

            def st_scores():
                for hh in range(2):
                    s_ = 2 * pr + hh
                    prt = slice(32 * s_, 32 * s_ + 32)
                    pairs = [(qbuf[prt, 2 * j, 128 * i:128 * i + 128], kbuf[prt, 2 * g, 128 * i:128 * i + 256]),
                             (qbuf[prt, 2 * j + 1, 128 * i:128 * i + 128], kbuf[prt, 2 * g + 1, 128 * i:128 * i + 256])]
                    mm_group(ps[:, bS[hh], 0:256], pairs,
                             [R(("q", 2 * j)), R(("q", 2 * j + 1)),
                              R(("k", 2 * g), 128 * i, 128 * i + 256), R(("k", 2 * g + 1), 128 * i, 128 * i + 256)],
                             [rS[hh]], tile_position=(32 * s_, 0))

            def st_mask():
                for hh in range(2):
                    simple("dve", lambda e, hh=hh: e.tensor_tensor(
                        out=sm[:, w, hh, 0:256], in0=ps[:, bS[hh], 0:256], in1=masks[:, mk, :], op=ALU.add),
                        [rS[hh], R("masks")], [R(("sm", w, hh))])
                simple("dve", lambda e: e.tensor_copy(out=sm[:, w, :, 256], in_=negsink[:, hcol:hcol + 2]),
                       [R("negsink")], [R(("smk", w))])

            def st_max():
                simple("dve", lambda e: e.tensor_reduce(out=stat[:, w, 0:2], in_=sm[:, w, :, :], axis=AX.X, op=ALU.max),
                       [R(("sm", w, 0)), R(("sm", w, 1)), R(("smk", w))], [R(("st", w, 0))])

            def st_negm():
                simple("dve", lambda e: e.tensor_scalar(
                    out=stat[:, w, 2:4], in0=stat[:, w, 0:2], scalar1=-0.125, scalar2=None, op0=ALU.mult),
                    [R(("st", w, 0))], [R(("st", w, 1))])

            def st_exp():
                for hh in range(2):
                    simple("act", lambda e, hh=hh: e.activation(
                        out=sm[:, w, hh, :], in_=sm[:, w, hh, :], func=AF.Exp, bias=stat[:, w, 2 + hh:3 + hh],
                        scale=0.125, accum_out=stat[:, w, 4 + hh:5 + hh]),
                        [R(("sm", w, hh)), R(("smk", w)), R(("st", w, 1))],
                        [R(("sm", w, hh)), R(("smk", w)), R(("st", w, 2 + hh))])

            def st_sinkadd():
                pass

            def st_sinkexp():
                pass

            def st_den():
                simple("dve", lambda e: e.reciprocal(out=stat[:, w, 6:8], in_=stat[:, w, 4:6]),
                       [R(("st", w, 2)), R(("st", w, 3))], [R(("st", w, 7))])

            def st_scale():
                simple("dve", lambda e: e.tensor_scalar(
                    out=pp[:, w, 0, :], in0=sm[:, w, 0, 0:256], scalar1=stat[:, w, 6:7], scalar2=None, op0=ALU.mult),
                    [R(("sm", w, 0)), R(("st", w, 7))], [R(("pp", w, 0))])
                simple("dve", lambda e: e.tensor_scalar(
                    out=pp[:, w, 1, :], in0=sm[:, w, 1, 0:256], scalar1=stat[:, w, 7:8], scalar2=None, op0=ALU.mult),
                    [R(("sm", w, 1)), R(("st", w, 7))], [R(("pp", w, 1))])

            def st_T():
                def tfn(e):
                    ins = None
                    for hh in range(2):
                        for half in range(2):
                            k0 = (hh * 2 + half) * 128
                            ins = e.transpose(psTv[:, k0:k0 + 128], pp[:, w, hh, half * 128:half * 128 + 128], ident[:])
                    return ins
                P.add("pe", tfn, reads=[R(("pp", w, 0)), R(("pp", w, 1)), R("ident")], writes=[rT])

            def st_Tcopy():
                simple("act", lambda e: e.activation(out=pT[:, w, :, :], in_=psTv[:, :], func=AF.Copy),
                       [rT], [R(("pT", w))])

            def st_PV():
                def pvfn(e):
                    ins = None
                    for hh in range(2):
                        for half in range(2):
                            ins = e.matmul(ps[64 * hh:64 * hh + 64, bS[1], 256:384],
                                           lhsT=vbuf[:, i + half, g * 64:g * 64 + 64],
                                           rhs=pT[:, w, hh * 2 + half, :],
                                           start=(half == 0), stop=(half == 1), tile_position=(0, 64 * hh),
                                           skip_group_check=True)
                    return ins
                P.add("pe", pvfn, reads=[R(("pT", w)), R(("v", i)), R(("v", i + 1))], writes=[rO])

            def st_Ocopy():
                simple("act", lambda e: e.activation(
                    out=G[:, ch, 128 * i:128 * i + 128], in_=ps[:, bS[1], 256:384], func=AF.Copy),
                    [rO], [R(("G", ch), 128 * i, 128 * i + 128)])

            return [st_scores, st_mask, st_max, st_negm, st_exp, st_sinkadd, st_sinkexp, st_den, st_scale,
                    st_T, st_Tcopy, st_PV, st_Ocopy]

        def l1_mixer(tile_t, T, o, acc):
            ka = 2 if tile_t == 0 else 0
            kc_base = 0 if tile_t == 0 else 128
            def ld_tab(e):
                e.dma_start(out=cosT[:], in_=cosd[:, 512 * tile_t: 512 * tile_t + KCOLS]).then_inc(ds_tab.sem, 16)
                return e.dma_start(out=sinT[:], in_=sind[:, 512 * tile_t: 512 * tile_t + KCOLS]).then_inc(ds_tab.sem, 16)
            P.add("sp", ld_tab, writes=[R("tab", 0, KCOLS)], dsem=ds_tab, ndma=2)
            if tile_t > 0:
                simple("act", lambda e: e.activation(out=kbuf[:, :, 0:128], in_=kbuf[:, :, 512:640], func=AF.Copy),
                       [R(("k", c), 512, 640) for c in range(8)], [R(("k", c), 0, 128) for c in range(8)])
                simple("act", lambda e: e.activation(out=vbuf[:, 0, :], in_=vbuf[:, 4, :], func=AF.Copy),
                       [R(("v", 4))], [R(("v", 0))])
            pp_flat = pp[:, :, :, :].rearrange("p a b c -> p (a b c)")
            kt = [pp_flat[:, 0:KCOLS], pp_flat[:, KCOLS:2 * KCOLS]]
            pp_all = [R(("pp", w4, h2)) for w4 in range(AW) for h2 in range(2)]
            slot, wres = wnext(tile_t, ("k", 0))
            first = True
            for (sa, sb_) in subtiles(ka, T):
                n = sb_ - sa
                kc0 = sa - ka + kc_base
                prs = [[(wring[:, slot, kc * 256 + half * 128: kc * 256 + half * 128 + 128], xn[:, kc, sa:sb_])
                        for kc in range(NCH)] for half in range(2)]
                if first:
                    bks = mm_staggered([(n, prs[half], [[R(("xn", kc), sa, sb_)] for kc in range(NCH)], [wres])
                                        for half in range(2)])
                    first = False
                else:
                    bks = []
                    for half in range(2):
                        bk = bank()
                        bks.append(bk)
                        mm_group(ps[:, bk, 0:n], prs[half], [wres] + [R(("xn", kc), sa, sb_) for kc in range(NCH)],
                                 [R(("ps", bk))])
                rope_pair(bks[0], bks[1], n, kc0, kt[0][:, kc0:kc0 + n], kt[1][:, kc0:kc0 + n],
                          [R(("kt", 0), kc0, kc0 + n)] + pp_all, [R(("kt", 1), kc0, kc0 + n)] + pp_all)
            c_lo = kc_base
            c_hi = KCOLS

            def krep(e):
                ins = None
                for g in range(4):
                    for ab in range(2):
                        for s_ in range(4):
                            ins = e.dma_start(out=kbuf[32 * s_:32 * s_ + 32, 2 * g + ab, c_lo:c_hi],
                                              in_=kt[ab][32 * g:32 * g + 32, c_lo:c_hi]).then_inc(ds_krep.sem, 16)
                return ins
            P.add("sp", krep, reads=[R(("kt", 0), c_lo, c_hi), R(("kt", 1), c_lo, c_hi)] + pp_all,
                  writes=[R(("k", c), c_lo, c_hi) for c in range(8)], dsem=ds_krep, ndma=32)
            slot, wres = wnext(tile_t, ("v",))
            for bb in (range(5) if tile_t == 0 else range(1, 5)):
                lt = (2 + 128 * bb) if tile_t == 0 else 128 * (bb - 1)
                bk = bank()
                pairs = [(xn[:, kc, lt:lt + 128], wring[:, slot, kc * 256: kc * 256 + 256]) for kc in range(NCH)]
                mm_group(ps[:, bk, 0:256], pairs, [wres] + [R(("xn", kc), lt, lt + 128) for kc in range(NCH)],
                         [R(("ps", bk))])
                simple("act", lambda e, bk=bk, bb=bb: e.activation(out=vbuf[:, bb, :], in_=ps[:, bk, 0:256], func=AF.Copy),
                       [R(("ps", bk))], [R(("v", bb))])
            for j in range(8):
                slot, wres = wnext(tile_t, ("q", j))
                bks = []
                for half in range(2):
                    bk = bank()
                    bks.append(bk)
                    pairs = [(wring[:, slot, kc * 256 + half * 128: kc * 256 + half * 128 + 128], xn[:, kc, o:o + TQ])
                             for kc in range(NCH)]
                    mm_group(ps[:, bk, 0:TQ], pairs, [wres] + [R(("xn", kc), o, o + TQ) for kc in range(NCH)],
                             [R(("ps", bk))])
                rope_pair(bks[0], bks[1], TQ, 128, qbuf[:, 2 * j, :], qbuf[:, 2 * j + 1, :],
                          R(("q", 2 * j)), R(("q", 2 * j + 1)))
            steps = []
            sidx = 0
            for i in range(4):
                mk = 0 if (tile_t == 0 and i == 0) else 1
                for j in range(8):
                    for pr in range(2):
                        steps.append(att_step(sidx, i, j, pr, mk))
                        sidx += 1
            NS = len(steps[0])
            DLT = -(-NS // AW)
            for tau in range((len(steps) - 1) * DLT + NS):
                for s_i in range(max(0, (tau - NS) // DLT), min(len(steps), tau // DLT + 1)):
                    k = tau - s_i * DLT
                    if 0 <= k < NS:
                        steps[s_i][k]()
            for i in range(8):
                slot, wres = wnext(tile_t, ("wo", i))
                for cc in range(2):
                    m2 = 2 * i + cc
                    bk = bank()
                    pairs = [(wring[:, slot, kc * 256 + cc * 128: kc * 256 + cc * 128 + 128], G[:, kc, 0:TQ])
                             for kc in range(NCH)]
                    mm_group(ps[:, bk, 0:TQ], pairs, [wres] + [R(("G", kc), 0, TQ) for kc in range(NCH)], [R(("ps", bk))])
                    simple("dve", lambda e, bk=bk, m2=m2: e.tensor_tensor(
                        out=h[:, m2, o:o + TQ], in0=ps[:, bk, 0:TQ], in1=h[:, m2, o:o + TQ], op=ALU.add),
                        [R(("ps", bk)), R(("h", m2), o, o + TQ)], [R(("h", m2), o, o + TQ)])
                    acc.flush(keep=1)
                    acc.feed(m2)

        ds_xp = DSem(sem("dxp"))
        stage_ap = []
        stage_res = []
        for c in range(8):
            stage_ap.append(qbuf[:, 2 * c:2 * c + 2, :].rearrange("p a b -> p (a b)").bitcast(F32))
            stage_res.append([R(("q", 2 * c)), R(("q", 2 * c + 1))])
        g_hi = G[:, 8:16, :].rearrange("p a b -> p (a b)").bitcast(F32)
        for k5 in range(5):
            stage_ap.append(g_hi[:, k5 * TQ:(k5 + 1) * TQ])
            stage_res.append([R(("G", ch), 0, TMAX) for ch in range(8, 16)])
        sm_flat = sm[:, :, :, :].rearrange("p a b c -> p (a b c)")
        for k3 in range(3):
            stage_ap.append(sm_flat[:, k3 * TQ:(k3 + 1) * TQ])
            stage_res.append([R(("sm", w4, h2)) for w4 in range(AW) for h2 in range(2)] + [R(("smk", w4)) for w4 in range(AW)])

        def prefetch_x(next_t):
            c0 = HALO + TQ * next_t

            def fn(e):
                ins = None
                for c in range(NCH):
                    ins = e.dma_start(out=stage_ap[c], in_=xT[c, :, c0:c0 + TQ]).then_inc(ds_xp.sem, 16)
                return ins
            P.add("sp", fn, writes=[r for rs in stage_res for r in rs], dsem=ds_xp, ndma=NCH)

        def unstage_x():
            for c in range(NCH):
                simple("act", lambda e, c=c: e.activation(out=h[:, c, 0:TQ], in_=stage_ap[c], func=AF.Copy),
                       stage_res[c], [R(("h", c), 0, TQ)])

        octr = [0]

        ds_cp = [DSem(sem(f"dcp{c}")) for c in range(NCH)]

        def write_out(tile_t, o, normed, refill=False):
            thunks = []
            order = [8, 9, 10, 11, 12] + list(range(8)) + [13, 14, 15] if refill else list(range(NCH))
            for c in order:
                def th(c=c):
                    oi = octr[0] % 2
                    octr[0] += 1
                    if normed:
                        simple("dve", lambda e: e.scalar_tensor_tensor(
                            out=scr[:, 2 + oi, :], in0=h[:, c, o:o + TQ], scalar=gain(4, c), in1=rstd[:, o:o + TQ],
                            op0=ALU.mult, op1=ALU.mult),
                            [R(("h", c), o, o + TQ), R("rstd", o, o + TQ), R("cst")], [R(("scr", 2 + oi))])
                    else:
                        simple("dve", lambda e: e.tensor_copy(out=scr[:, 2 + oi, :], in_=h[:, c, o:o + TQ]),
                               [R(("h", c), o, o + TQ)], [R(("scr", 2 + oi))])
                    P.add("sp", lambda e: e.dma_start(
                        out=outT[c, :, TQ * tile_t: TQ * tile_t + TQ], in_=scr[:, 2 + oi, :]).then_inc(ds_out[oi].sem, 16),
                        reads=[R(("scr", 2 + oi))], writes=[R(("out", tile_t, c))], dsem=ds_out[oi], ndma=1)
                    if refill:
                        P.add("sp", lambda e: e.dma_start(out=h[:, c, 0:TQ], in_=stage_ap[c]).then_inc(ds_cp[c].sem, 16),
                              reads=stage_res[c], writes=[R(("h", c), 0, TQ)], dsem=ds_cp[c], ndma=1)
                thunks.append(th)
            return thunks

        pending_out = None
        for t in range(NT_TILES):
            P.epoch = t
            T = TMAX if t == 0 else TQ
            o = HALO if t == 0 else 0
            c0 = 0 if t == 0 else HALO + TQ * t

            def ldx(e, T=T, c0=c0):
                ins = None
                for c in range(NCH):
                    ins = e.dma_start(out=h[:, c, 0:T], in_=xT[c, :, c0:c0 + T]).then_inc(ds_x.sem, 16)
                return ins
            ka = 2 if t == 0 else 0
            if t == 0 or dbg:
                P.add("sp", ldx, writes=[R(("h", c), 0, T) for c in range(NCH)], dsem=ds_x, ndma=NCH)
                acc = StatAcc(0, T)
                for c in range(NCH):
                    acc.feed(c)
                    acc.flush(keep=1)
                acc.finish()
                norm_apply_xn(0, 0, T)
            acc = StatAcc(0, T)
            l0_mixer(t, T, acc, side=pending_out)
            pending_out = None
            acc.finish()
            if dbg == "l0mix":
                wctr[0] = (t + 1) * NWT
                for th in write_out(t, o, False):
                    th()
                continue
            norm_apply_xn(1, 0, T)
            acc = StatAcc(ka, T)
            mlp(t, 0, T, 0, acc)
            acc.finish()
            if dbg == "l0":
                wctr[0] = (t + 1) * NWT
                for th in write_out(t, o, False):
                    th()
                continue
            norm_apply_xn(2, ka, T)
            acc = StatAcc(o, o + TQ)
            l1_mixer(t, T, o, acc)
            acc.finish()
            if dbg == "l1mix":
                wctr[0] = (t + 1) * NWT
                for th in write_out(t, o, False):
                    th()
                continue
            norm_apply_xn(3, o, o + TQ)
            hooks = None
            accN = None
            if t + 1 < NT_TILES:
                prefetch_x(t + 1)
                stage_src = lambda c: (stage_ap[c], stage_res[c])
                accN = StatAcc(0, TQ, banks=[7], src=stage_src, rdst=(scr[:, 0, :], ("scr", 0)))

                def early_stats(accN=accN):
                    for c in range(NCH):
                        accN.feed(c)
                        accN.flush(keep=1)
                    accN.flush(0)
                hooks = {12: early_stats}
            acc = StatAcc(o, o + TQ)
            mlp(t, o, o + TQ, 1, acc, hooks)
            acc.finish()
            assert wctr[0] == (t + 1) * NWT, (wctr[0], t)
            if accN is not None:
                accN.finish()
                norm_apply_xn(0, 0, TQ, src=stage_src, rs=(scr[:, 0, :], ("scr", 0)))
                pending_out = write_out(t, o, True, refill=True)
            else:
                for th in write_out(t, o, True):
                    th()
        P.epoch = NT_TILES
        P.add("sp", lambda e: None, reads=[R(("out", t, c)) for t in range(NT_TILES) for c in range(NCH)])
        P.finalize_and_emit(nc, sems)
    return nc, PLAN


def _tile16(W, cols):
    sub = W[:, cols]
    n = sub.shape[1]
    return sub.reshape(16, 128, n).transpose(1, 0, 2).reshape(128, 16 * n)


def build_weight_stream(plan, w_in_conv, w_out_conv, w_up_0, w_down_0, w_qkv, w_o, w_up_1, w_down_1):
    wts = np.zeros((len(plan), 128, SLOT), np.float32)
    ar = np.arange
    ups = [w_up_0, w_up_1]
    downs = [w_down_0, w_down_1]
    win_cache = {}
    for n, key in enumerate(plan):
        kind = key[0]
        if kind == "win":
            m, half = key[1], key[2]
            if m not in win_cache:
                win_cache.clear()
                cols = np.concatenate([m * 128 + ar(128), 2048 + m * 128 + ar(128), 4096 + m * 128 + ar(128)])
                win_cache[m] = w_in_conv[:, cols].reshape(16, 128, 384)
            sub = win_cache[m]
            wts[n, :, 0:3072] = sub[half * 8:(half + 1) * 8].transpose(1, 0, 2).reshape(128, 3072)
        elif kind == "wout":
            wts[n] = _tile16(w_out_conv, key[1] * 256 + ar(256))
        elif kind == "up":
            _, lyr, blk, ui = key
            wts[n] = _tile16(ups[lyr], blk * 512 + ui * 256 + ar(256))
        elif kind == "down":
            _, lyr, blk, di = key
            r0 = (blk * 4 + di * 2) * 128
            wts[n] = downs[lyr][r0:r0 + 256].reshape(2, 128, 2048).transpose(1, 0, 2).reshape(128, 4096)
        elif kind == "k":
            a = np.concatenate([2048 + g * 64 + ar(32) for g in range(4)])
            b = np.concatenate([2048 + g * 64 + 32 + ar(32) for g in range(4)])
            wts[n] = _tile16(w_qkv, np.concatenate([a, b]))
        elif kind == "v":
            wts[n] = _tile16(w_qkv, 2304 + ar(256))
        elif kind == "q":
            j = key[1]
            a = np.concatenate([(4 * j + s) * 64 + ar(32) for s in range(4)])
            b = np.concatenate([(4 * j + s) * 64 + 32 + ar(32) for s in range(4)])
            wts[n] = _tile16(w_qkv, np.concatenate([a, b]))
        elif kind == "wo":
            wts[n] = _tile16(w_o, key[1] * 256 + ar(256))
        else:
            raise ValueError(key)
    return wts


_NC_CACHE = {}


def kernel(x, meta_tokens, norm_mix_0, w_in_conv, conv_w, w_out_conv, norm_mlp_0, w_up_0, w_down_0,
           norm_mix_1, w_qkv, attn_sinks, w_o, norm_mlp_1, w_up_1, w_down_1, norm_final, _dbg=""):
    f = lambda a: np.asarray(a, dtype=np.float32)
    x = f(x)
    B = x.shape[0]
    if _dbg not in _NC_CACHE:
        _NC_CACHE[_dbg] = build_program(_dbg)
    nc, plan = _NC_CACHE[_dbg]
    assert len(plan) == NWT or _dbg
    wts = build_weight_stream(plan, f(w_in_conv), f(w_out_conv), f(w_up_0), f(w_down_0), f(w_qkv), f(w_o), f(w_up_1),
                              f(w_down_1))
    if wts.shape[0] < NWT:
        wts = np.concatenate([wts, np.zeros((NWT - wts.shape[0], 128, SLOT), np.float32)], axis=0)
    cst = np.zeros((128, 160), np.float32)
    for n_, gvec in enumerate([norm_mix_0, norm_mlp_0, norm_mix_1, norm_mlp_1, norm_final]):
        cst[:, n_ * 16:(n_ + 1) * 16] = f(gvec).reshape(16, 128).T
    for k in range(3):
        cst[:, 80 + k * 16: 80 + (k + 1) * 16] = f(conv_w)[k].reshape(16, 128).T
    cst[:, 128:160] = f(attn_sinks)[None, :]
    ident = np.eye(128, dtype=np.float32).astype(ml_dtypes.bfloat16)
    inv = (np.float32(10000.0) ** (-np.arange(0, 64, 2, dtype=np.float32) / np.float32(64))).astype(np.float32)
    sgn = np.ones((128, 1), np.float32)
    in_maps = []
    for c in range(8):
        b, j = divmod(c, 4)
        lo = j * TOK_CORE
        hfull = np.concatenate([np.zeros((114, D), np.float32), f(meta_tokens), x[b]], axis=0)
        xTc = np.ascontiguousarray(hfull[lo:lo + XCOLS].T).reshape(NCH, 128, XCOLS)
        pos = (np.arange(2176, dtype=np.float32) + np.float32(lo - 112)).astype(np.float32)
        ang = (pos[None, :] * inv[:, None]).astype(np.float32)
        cosd = np.tile(np.cos(ang).astype(np.float32), (4, 1))
        sind = np.tile(np.sin(ang).astype(np.float32), (4, 1))
        qq = np.arange(128)[:, None]
        kk = np.arange(256)[None, :]
        allowed = (kk > qq) & (kk <= qq + 128)
        m1 = np.where(allowed, 0.0, NEG).astype(np.float32)
        m0 = np.where(allowed & ((kk >= 112) | (j != 0)), 0.0, NEG).astype(np.float32)
        maskd = np.ascontiguousarray(np.stack([m0, m1], axis=1))
        in_maps.append({"xT": xTc, "wts": wts, "cosd": np.ascontiguousarray(cosd), "sind": np.ascontiguousarray(sind),
                        "maskd": maskd, "cstd": cst, "identd": ident})
    res = run_bass_kernel_spmd(nc, in_maps, core_ids=list(range(8)))
    out = np.empty((B, SEQ, D), np.float32)
    for c in range(8):
        b, j = divmod(c, 4)
        oT = res.results[c]["outT"].reshape(D, TOK_CORE)
        out[b, j * TOK_CORE:(j + 1) * TOK_CORE, :] = oT.T
    return out
```
